# Optimizing a Trainium2 kernel written in Bass

```python
import math
import jax, jax.numpy as jnp
from jax import lax
import numpy as np

D_MODEL = 1024
BATCH = 32
SEQ = 2048
DEPTH = 4

CTX_LEN = 256
GRID_W = 64
N_MIXERS = 3
MIXER_POOL = 0
MIXER_SSM = 1
MIXER_NAT = 2
N_POOL_LAYERS = (DEPTH + 2) // 3
N_SSM_LAYERS = (DEPTH + 1) // 3
N_NAT_LAYERS = DEPTH // 3
POOL_WINDOWS = (2, 4, 8, 16)
POOL_GROUP = D_MODEL // len(POOL_WINDOWS)
SSM_GROUP = 16
SSM_GROUPS = D_MODEL // SSM_GROUP
SSM_STATE = 64
SSM_DT_MIN = 1e-3
SSM_DT_MAX = 1e-1
SSM_MAX_REAL = -1e-4
NAT_HEADS = 16
NAT_HEAD_DIM = D_MODEL // NAT_HEADS
NAT_ROWS = 8
NAT_COLS = 16
D_FF = 2816
FFN_CONV = 3
RMS_EPS = 1e-6
NEG_INF = -1e30

kernel_name = "hybrid_pool_s5_natten_dit"


def _ctx_read_after(i):
    return any((j % N_MIXERS) != MIXER_POOL for j in range(i + 1, DEPTH))


def _rmsnorm(x, g):
    x32 = x.astype(jnp.float32)
    y = x32 * lax.rsqrt(jnp.mean(x32 * x32, axis=-1, keepdims=True) + RMS_EPS)
    return (y * g.astype(jnp.float32)).astype(x.dtype)


def _modulate(x, shift, scale):
    return x * (1 + scale) + shift


def _pool_mixer(u, w, b, scale):
    bsz, n, _ = u.shape
    ng = len(POOL_WINDOWS)
    u32 = u.astype(jnp.float32).reshape(bsz, n, ng, POOL_GROUP)
    csum = jnp.pad(jnp.cumsum(u32, axis=1), ((0, 0), (1, 0), (0, 0), (0, 0)))
    t = np.arange(n)
    parts = []
    for gi, win in enumerate(POOL_WINDOWS):
        lo = np.clip(t - win // 2, 0, n - 1)
        hi = np.clip(t + win - 1 - win // 2, 0, n - 1)
        cnt = (hi - lo + 1).astype(np.float32)[None, :, None]
        cs = csum[:, :, gi]
        mean = (jnp.take(cs, hi + 1, axis=1) - jnp.take(cs, lo, axis=1)) / cnt
        parts.append(mean - u32[:, :, gi])
    p = jnp.stack(parts, axis=2)
    y = jnp.einsum('blgc,gcd->blgd', p, w.astype(jnp.float32)) + b.astype(jnp.float32).reshape(ng, POOL_GROUP)
    return (y.reshape(bsz, n, D_MODEL) * scale.astype(jnp.float32)).astype(u.dtype)


def _s5_discretise(a_re, a_im, log_dt, b_re, b_im):
    lam = lax.complex(jnp.minimum(a_re.astype(jnp.float32), SSM_MAX_REAL), a_im.astype(jnp.float32))
    dt = jnp.exp(log_dt.astype(jnp.float32))[:, None]
    a_bar = jnp.exp(lam * dt)
    b_bar = ((a_bar - 1.0) / lam)[:, :, None] * lax.complex(b_re.astype(jnp.float32), b_im.astype(jnp.float32))
    return a_bar, b_bar


def _scan_op(e1, e2):
    a1, h1 = e1
    a2, h2 = e2
    return a1 * a2, a2 * h1 + h2


def _diag_scan(a_bar, bu, reverse):
    a = jnp.broadcast_to(a_bar, (1, bu.shape[1]) + a_bar.shape)
    _, h = lax.associative_scan(_scan_op, (a, bu), reverse=reverse, axis=1)
    return h


def _s5_mixer(u_lat, u_ctx, a_re, a_im, log_dt, b_re, b_im, c_re, c_im, d_skip, w1, b1, w2, b2, need_ctx_out):
    def grouped(u):
        return u.astype(jnp.float32).reshape(u.shape[0], u.shape[1], SSM_GROUPS, SSM_GROUP).astype(jnp.complex64)

    ul = grouped(u_lat)
    uc = grouped(u_ctx)
    y_lat = []
    y_ctx = []
    for direction in range(2):
        reverse = direction == 1
        a_bar, b_bar = _s5_discretise(a_re[direction], a_im[direction], log_dt[direction],
                                      b_re[direction], b_im[direction])
        c_cplx = lax.complex(c_re[direction].astype(jnp.float32), c_im[direction].astype(jnp.float32))
        h_ctx = _diag_scan(a_bar, jnp.einsum('blgc,gpc->blgp', uc, b_bar), reverse)
        h0 = h_ctx[:, 0] if reverse else h_ctx[:, -1]
        bu = jnp.einsum('blgc,gpc->blgp', ul, b_bar)
        entry = -1 if reverse else 0
        bu = bu.at[:, entry].add(a_bar * h0)
        h_lat = _diag_scan(a_bar, bu, reverse)
        y_lat.append(jnp.real(jnp.einsum('blgp,gcp->blgc', h_lat, c_cplx)))
        if need_ctx_out:
            y_ctx.append(jnp.real(jnp.einsum('blgp,gcp->blgc', h_ctx, c_cplx)))

    def glu_out(y, u):
        bsz, n, _ = u.shape
        y = y.reshape(bsz, n, D_MODEL) + d_skip.astype(jnp.float32) * u.astype(jnp.float32)
        g = jax.nn.gelu(y).astype(u.dtype)
        return ((g @ w1 + b1) * jax.nn.sigmoid(g @ w2 + b2)).astype(u.dtype)

    out_lat = glu_out(y_lat[0] + y_lat[1], u_lat)
    out_ctx = glu_out(y_ctx[0] + y_ctx[1], u_ctx) if need_ctx_out else None
    return out_lat, out_ctx


def _nat_mixer(u_lat, u_ctx, w_qkv, w_o, rpb, need_ctx_out):
    bsz, n, _ = u_lat.shape
    n_ctx = u_ctx.shape[1]
    rows = n // GRID_W
    kr = min(NAT_ROWS, rows)
    scale = NAT_HEAD_DIM ** -0.5
    qkv = (u_lat @ w_qkv).reshape(bsz, rows, GRID_W, 3, NAT_HEADS, NAT_HEAD_DIM)
    q = qkv[:, :, :, 0] * scale
    k = qkv[:, :, :, 1]
    v = qkv[:, :, :, 2]
    kvc = (u_ctx @ w_qkv[:, D_MODEL:]).reshape(bsz, n_ctx, 2, NAT_HEADS, NAT_HEAD_DIM)
    kc = kvc[:, :, 0]
    vc = kvc[:, :, 1]

    r = np.arange(rows)
    row_idx = np.clip(r - kr // 2, 0, rows - kr)[:, None] + np.arange(kr)[None, :]
    k_band = k[:, row_idx]
    v_band = v[:, row_idx]
    col = np.arange(GRID_W)
    col_start = np.clip(col - NAT_COLS // 2, 0, GRID_W - NAT_COLS)
    col_mask = (col[None, :] >= col_start[:, None]) & (col[None, :] < col_start[:, None] + NAT_COLS)
    row_off = row_idx - r[:, None] + (NAT_ROWS - 1)
    col_off = np.clip(col[None, :] - col[:, None], 1 - NAT_COLS, NAT_COLS - 1) + (NAT_COLS - 1)
    bias = rpb.astype(jnp.float32)[:, row_off][:, :, :, col_off]
    bias = jnp.transpose(bias, (1, 0, 3, 2, 4))

    s_loc = jnp.einsum('brihd,brjwhd->brhijw', q, k_band).astype(jnp.float32) + bias
    s_loc = jnp.where(col_mask[:, None, :], s_loc, NEG_INF)
    s_ctx = jnp.einsum('brihd,bchd->brhic', q, kc).astype(jnp.float32)
    n_loc = kr * GRID_W
    s = jnp.concatenate([s_loc.reshape(bsz, rows, NAT_HEADS, GRID_W, n_loc), s_ctx], axis=-1)
    p = jax.nn.softmax(s, axis=-1).astype(v.dtype)
    p_loc = p[..., :n_loc].reshape(bsz, rows, NAT_HEADS, GRID_W, kr, GRID_W)
    o = (jnp.einsum('brhijw,brjwhd->brihd', p_loc, v_band)
         + jnp.einsum('brhic,bchd->brihd', p[..., n_loc:], vc))
    out_lat = o.reshape(bsz, n, D_MODEL) @ w_o

    out_ctx = None
    if need_ctx_out:
        qc = (u_ctx @ w_qkv[:, :D_MODEL]).reshape(bsz, n_ctx, NAT_HEADS, NAT_HEAD_DIM) * scale
        sc = jnp.einsum('bqhd,bkhd->bhqk', qc, kc).astype(jnp.float32)
        pc = jax.nn.softmax(sc, axis=-1).astype(vc.dtype)
        oc = jnp.einsum('bhqk,bkhd->bqhd', pc, vc)
        out_ctx = oc.reshape(bsz, n_ctx, D_MODEL) @ w_o
    return out_lat, out_ctx


def _conv_ffn(u, w_up, conv_w, conv_b, w_down):
    hid = u @ w_up
    hid = lax.conv_general_dilated(hid, conv_w[:, None, :].astype(hid.dtype), window_strides=(1,),
                                   padding=((FFN_CONV // 2, FFN_CONV // 2),),
                                   dimension_numbers=('NWC', 'WIO', 'NWC'),
                                   feature_group_count=2 * D_FF) + conv_b
    val, gate = jnp.split(hid, 2, axis=-1)
    return (jax.nn.silu(gate) * val) @ w_down


def setup_inputs(seed: int = 0) -> dict:
    key = jax.random.key(seed)
    ks = iter(jax.random.split(key, 48))
    f32 = jnp.float32

    def nrm(shape, s):
        return jax.random.normal(next(ks), shape, f32) * s

    D = D_MODEL
    F2 = 2 * D_FF
    x = nrm((BATCH, SEQ, D), 1.0)
    c = nrm((BATCH, D), 1.0)
    ctx = nrm((BATCH, CTX_LEN, D), 1.0)
    c_ctx = nrm((D,), 1.0)
    norm1_g = 1.0 + nrm((DEPTH, D), 0.05)
    norm2_g = 1.0 + nrm((DEPTH, D), 0.05)
    ada_w = nrm((DEPTH, D, 6 * D), 0.5 * D ** -0.5)
    ada_b = nrm((DEPTH, 6 * D), 0.02)
    ffn_w_up = nrm((DEPTH, D, F2), D ** -0.5)
    ffn_conv_w = nrm((DEPTH, FFN_CONV, F2), FFN_CONV ** -0.5)
    ffn_conv_b = nrm((DEPTH, F2), 0.02)
    ffn_w_down = nrm((DEPTH, D_FF, D), D_FF ** -0.5)
    pool_w = nrm((N_POOL_LAYERS, len(POOL_WINDOWS), POOL_GROUP, POOL_GROUP), POOL_GROUP ** -0.5)
    pool_b = nrm((N_POOL_LAYERS, D), 0.02)
    pool_scale = 1.0 + nrm((N_POOL_LAYERS, D), 0.1)
    ssm_shape = (N_SSM_LAYERS, 2, SSM_GROUPS, SSM_STATE)
    s5_a_re = -0.5 * jnp.exp(nrm(ssm_shape, 0.05))
    s5_a_im = math.pi * jnp.arange(SSM_STATE, dtype=f32) + nrm(ssm_shape, 0.05)
    s5_log_dt = jax.random.uniform(next(ks), (N_SSM_LAYERS, 2, SSM_GROUPS), f32,
                                   math.log(SSM_DT_MIN), math.log(SSM_DT_MAX))
    s5_b_re = nrm((N_SSM_LAYERS, 2, SSM_GROUPS, SSM_STATE, SSM_GROUP), (2 * SSM_GROUP) ** -0.5)
    s5_b_im = nrm((N_SSM_LAYERS, 2, SSM_GROUPS, SSM_STATE, SSM_GROUP), (2 * SSM_GROUP) ** -0.5)
    s5_c_re = nrm((N_SSM_LAYERS, 2, SSM_GROUPS, SSM_GROUP, SSM_STATE), (2 * SSM_STATE) ** -0.5 * 4.0)
    s5_c_im = nrm((N_SSM_LAYERS, 2, SSM_GROUPS, SSM_GROUP, SSM_STATE), (2 * SSM_STATE) ** -0.5 * 4.0)
    s5_d = nrm((N_SSM_LAYERS, D), 1.0)
    s5_w1 = nrm((N_SSM_LAYERS, D, D), D ** -0.5)
    s5_b1 = nrm((N_SSM_LAYERS, D), 0.02)
    s5_w2 = nrm((N_SSM_LAYERS, D, D), D ** -0.5)
    s5_b2 = nrm((N_SSM_LAYERS, D), 0.02)
    nat_w_qkv = nrm((N_NAT_LAYERS, D, 3 * D), D ** -0.5)
    nat_w_o = nrm((N_NAT_LAYERS, D, D), D ** -0.5)
    nat_rpb = nrm((N_NAT_LAYERS, NAT_HEADS, 2 * NAT_ROWS - 1, 2 * NAT_COLS - 1), 0.02)
    final_g = 1.0 + nrm((D,), 0.05)
    return {"x": x, "c": c, "ctx": ctx, "c_ctx": c_ctx,
            "norm1_g": norm1_g, "norm2_g": norm2_g, "ada_w": ada_w, "ada_b": ada_b,
            "ffn_w_up": ffn_w_up, "ffn_conv_w": ffn_conv_w, "ffn_conv_b": ffn_conv_b, "ffn_w_down": ffn_w_down,
            "pool_w": pool_w, "pool_b": pool_b, "pool_scale": pool_scale,
            "s5_a_re": s5_a_re, "s5_a_im": s5_a_im, "s5_log_dt": s5_log_dt,
            "s5_b_re": s5_b_re, "s5_b_im": s5_b_im, "s5_c_re": s5_c_re, "s5_c_im": s5_c_im,
            "s5_d": s5_d, "s5_w1": s5_w1, "s5_b1": s5_b1, "s5_w2": s5_w2, "s5_b2": s5_b2,
            "nat_w_qkv": nat_w_qkv, "nat_w_o": nat_w_o, "nat_rpb": nat_rpb,
            "final_g": final_g}


def reference(x, c, ctx, c_ctx, norm1_g, norm2_g, ada_w, ada_b, ffn_w_up, ffn_conv_w, ffn_conv_b, ffn_w_down,
              pool_w, pool_b, pool_scale, s5_a_re, s5_a_im, s5_log_dt, s5_b_re, s5_b_im, s5_c_re, s5_c_im,
              s5_d, s5_w1, s5_b1, s5_w2, s5_b2, nat_w_qkv, nat_w_o, nat_rpb, final_g):
    h_lat = x
    h_ctx = ctx
    cond_lat = jax.nn.silu(c)
    cond_ctx = jax.nn.silu(c_ctx)
    for i in range(DEPTH):
        kind = i % N_MIXERS
        slot = i // N_MIXERS
        ctx_out = _ctx_read_after(i)
        ctx_in = ctx_out or kind != MIXER_POOL
        mod_lat = (cond_lat @ ada_w[i] + ada_b[i])[:, None, :]
        sh1, sc1, g1, sh2, sc2, g2 = jnp.split(mod_lat, 6, axis=-1)
        u_lat = _modulate(_rmsnorm(h_lat, norm1_g[i]), sh1, sc1)
        u_ctx = None
        if ctx_in:
            mod_ctx = cond_ctx @ ada_w[i] + ada_b[i]
            csh1, csc1, cg1, csh2, csc2, cg2 = jnp.split(mod_ctx, 6, axis=-1)
            u_ctx = _modulate(_rmsnorm(h_ctx, norm1_g[i]), csh1, csc1)

        if kind == MIXER_POOL:
            y_lat = _pool_mixer(u_lat, pool_w[slot], pool_b[slot], pool_scale[slot])
            y_ctx = _pool_mixer(u_ctx, pool_w[slot], pool_b[slot], pool_scale[slot]) if ctx_out else None
        elif kind == MIXER_SSM:
            y_lat, y_ctx = _s5_mixer(u_lat, u_ctx, s5_a_re[slot], s5_a_im[slot], s5_log_dt[slot],
                                     s5_b_re[slot], s5_b_im[slot], s5_c_re[slot], s5_c_im[slot],
                                     s5_d[slot], s5_w1[slot], s5_b1[slot], s5_w2[slot], s5_b2[slot], ctx_out)
        else:
            y_lat, y_ctx = _nat_mixer(u_lat, u_ctx, nat_w_qkv[slot], nat_w_o[slot], nat_rpb[slot], ctx_out)

        h_lat = h_lat + g1 * y_lat
        v_lat = _modulate(_rmsnorm(h_lat, norm2_g[i]), sh2, sc2)
        h_lat = h_lat + g2 * _conv_ffn(v_lat, ffn_w_up[i], ffn_conv_w[i], ffn_conv_b[i], ffn_w_down[i])
        if ctx_out:
            h_ctx = h_ctx + cg1 * y_ctx
            v_ctx = _modulate(_rmsnorm(h_ctx, norm2_g[i]), csh2, csc2)
            h_ctx = h_ctx + cg2 * _conv_ffn(v_ctx, ffn_w_up[i], ffn_conv_w[i], ffn_conv_b[i], ffn_w_down[i])
    return _rmsnorm(h_lat, final_g)
```

```python
import numpy as np
import concourse.bass as bass
import concourse.mybir as mybir
from concourse.bass_utils import run_bass_kernel_spmd

F32 = mybir.dt.float32
BF16 = mybir.dt.bfloat16
AF = mybir.ActivationFunctionType
ALU = mybir.AluOpType

D = 1024
KC = 8
SEQ = 2048
CTX = 256
NTOK = SEQ + CTX
DEPTH = 4
DFF = 2816
NFC = 22
POOL_WINDOWS = (2, 4, 8, 16)
EPS = 1e-6
LAT0 = 8
CTX0 = 8 + SEQ + 16
UCOLS = CTX0 + CTX + 8


class Res:
    __slots__ = ("name", "w", "r", "dsem")

    def __init__(self, name):
        self.name = name
        self.w = None
        self.r = []
        self.dsem = None


class EngState:
    def __init__(self, name, h, sem):
        self.name = name
        self.h = h
        self.sem = sem
        self.count = 0
        self.waited = {}


class Ctx:
    def __init__(self, nc, sems):
        self.nc = nc
        self.free_sems = list(sems)
        self.E = {}
        for name, h in (("pe", nc.tensor), ("act", nc.scalar), ("dve", nc.vector),
                        ("pool", nc.gpsimd), ("sp", nc.sync)):
            self.E[name] = EngState(name, h, self.free_sems.pop())
        self.dtot = {}
        self.n_inst = 0

    def res(self, name):
        return Res(name)

    def _wait(self, e, tok):
        sem, val = tok
        if sem in self.dtot:
            val = self.dtot[sem]
        key = id(sem)
        if e.waited.get(key, 0) < val:
            e.h.wait_ge(sem, val)
            e.waited[key] = val

    def _deps(self, e, reads, writes):
        for r in reads:
            if r.w is not None:
                self._wait(e, r.w)
        for w in writes:
            if w.w is not None:
                self._wait(e, w.w)
            for t in w.r:
                self._wait(e, t)

    def _register(self, tok, reads, writes):
        for r in reads:
            r.r.append(tok)
        for w in writes:
            w.w = tok
            w.r = []

    def op(self, eng, fn, reads=(), writes=()):
        e = self.E[eng]
        self._deps(e, reads, writes)
        inst = fn(e.h)
        e.count += 1
        inst.then_inc(e.sem, 1)
        self._register((e.sem, e.count), reads, writes)
        self.n_inst += 1
        return inst

    def mm_group(self, mms, reads, writes, transpose=False):
        e = self.E["pe"]
        self._deps(e, reads, writes)
        n = len(mms)
        inst = None
        for i, (o, l, r) in enumerate(mms):
            inst = e.h.matmul(o, l, r, start=(i == 0), stop=(i == n - 1))
        e.count += 1
        inst.then_inc(e.sem, 1)
        self._register((e.sem, e.count), reads, writes)
        self.n_inst += n

    def mm_multi(self, groups, reads, writes):
        e = self.E["pe"]
        self._deps(e, reads, writes)
        inst = None
        cnt = 0
        for mms in groups:
            n = len(mms)
            for i, (o, l, r) in enumerate(mms):
                inst = e.h.matmul(o, l, r, start=(i == 0), stop=(i == n - 1))
                cnt += 1
        e.count += 1
        inst.then_inc(e.sem, 1)
        self._register((e.sem, e.count), reads, writes)
        self.n_inst += cnt

    def barrier(self):
        for e in self.E.values():
            for o in self.E.values():
                if o is not e and o.count > 0:
                    self._wait(e, (o.sem, o.count))
            for sem, tot in self.dtot.items():
                if tot > 0:
                    self._wait(e, (sem, tot))

    def pe_raw(self, fns, reads, writes):
        e = self.E["pe"]
        self._deps(e, reads, writes)
        inst = None
        for fn in fns:
            inst = fn(e.h)
        e.count += 1
        inst.then_inc(e.sem, 1)
        self._register((e.sem, e.count), reads, writes)
        self.n_inst += len(fns)

    def dma(self, q, out, in_, reads=(), writes=(), **kw):
        e = self.E[q]
        self._deps(e, reads, writes)
        owner = writes[0] if writes else reads[0]
        if owner.dsem is None:
            owner.dsem = self.free_sems.pop()
            self.dtot[owner.dsem] = 0
        sem = owner.dsem
        self.dtot[sem] += 16
        e.h.dma_start(out=out, in_=in_, **kw).then_inc(sem, 16)
        self._register((sem, self.dtot[sem]), reads, writes)
        self.n_inst += 1

    def wait_all(self, eng, ress):
        e = self.E[eng]
        for r in ress:
            if r.w is not None:
                self._wait(e, r.w)
            for t in r.r:
                self._wait(e, t)


def _fm(v):
    v = np.asarray(v, np.float32)
    lead = v.shape[:-1]
    n = v.shape[-1] // 128
    v = v.reshape(lead + (n, 128))
    v = np.moveaxis(v, -1, 0)
    return np.ascontiguousarray(v).reshape(128, -1)


class VecPack:
    def __init__(self):
        self.cols = {}
        self.parts = []
        self.n = 0

    def add(self, name, arr):
        arr = np.ascontiguousarray(arr, np.float32).reshape(128, -1)
        self.cols[name] = (self.n, arr.shape[1])
        self.parts.append(arr)
        self.n += arr.shape[1]

    def pack(self):
        return np.ascontiguousarray(np.concatenate(self.parts, axis=1))


VEC_LAYOUT = None


def make_vecs(inp):
    vp = VecPack()
    vp.add("n1g", _fm(inp["norm1_g"]))
    vp.add("n2g", _fm(inp["norm2_g"]))
    vp.add("fg", _fm(inp["final_g"]))
    vp.add("adab", _fm(inp["ada_b"]))
    vp.add("cw", _fm(inp["ffn_conv_w"]))
    vp.add("cb", _fm(inp["ffn_conv_b"]))
    vp.add("pb", _fm(inp["pool_b"]))
    vp.add("ps", _fm(inp["pool_scale"]))
    vp.add("s5d", _fm(inp["s5_d"]))
    vp.add("s5b1", _fm(inp["s5_b1"]))
    vp.add("s5b2", _fm(inp["s5_b2"]))
    tbl = np.zeros((4, 2, 8), np.float32)
    for wi, w in enumerate(POOL_WINDOWS):
        for t in range(w // 2):
            tbl[wi, 0, t] = 1.0 / (w // 2 + t)
        for i in range(w // 2 - 1):
            tbl[wi, 1, i] = 1.0 / (w - 1 - i)
    vp.add("ptbl", np.broadcast_to(tbl.reshape(1, -1), (128, 64)))
    return vp


class Cfg:
    def __init__(self, nb=4, nl=4, final_norm=True):
        self.nb = nb
        self.nl = nl
        self.final_norm = final_norm


def build_program(cfg, vec_cols, nvec):
    nc = bass.Bass("TRN2", target_bir_lowering=False, dynamic_dma_scratch_size=4096)
    NB = cfg.nb

    def dram_in(name, shape, dt=F32):
        return nc.dram_tensor(name, list(shape), dt, kind="ExternalInput").ap()

    xT = dram_in("xT", [NB, D, SEQ])
    cT = dram_in("ctxT", [NB, D, CTX])
    cond = dram_in("cond", [128, KC * 5])
    vecs_d = dram_in("vecs", [128, nvec])
    ada_w = dram_in("ada_w", [DEPTH, D, 6 * D])
    wup = dram_in("wup", [DEPTH, NFC, 128, KC * 256])
    wdn = dram_in("wdn", [DEPTH, DFF, D])
    pool_w = dram_in("pool_w", [2, 4, 256, 256])
    s5a_d = dram_in("s5a", [128, 3 * 64])
    s5bc_d = dram_in("s5bc", [128, 4, 64, 16])
    cst_d = dram_in("cst", [128, 386])
    sel_d = dram_in("sel", [128, 64 * 128])
    s5w1_d = dram_in("s5w1", [D, D])
    s5w2_d = dram_in("s5w2", [D, D])
    nqkv_d = dram_in("nqkv", [D, 3 * D])
    nwo_d = dram_in("nwo", [D, D])
    btab_d = dram_in("btab", [16, 128, 21, 128])
    a16_d = nc.dram_tensor("a16_scr", [128, 256], F32, kind="Internal").ap()
    s5w_d = nc.dram_tensor("s5w_scr", [64, 128, 896], BF16, kind="Internal").ap()
    yT = nc.dram_tensor("yT", [NB, D, SEQ], F32, kind="ExternalOutput").ap()

    import contextlib
    es = contextlib.ExitStack()
    with es:
        sems = [es.enter_context(nc.semaphore(f"s{i}")) for i in range(100)]
        cx = Ctx(nc, sems)

        def sb(name, shape, dt):
            return es.enter_context(nc.sbuf_tensor(name, list(shape), dt))

        h = sb("h", [128, KC, NTOK], F32)
        R_h = [cx.res(f"h{k}") for k in range(KC)]
        u = sb("u", [128, KC, UCOLS], BF16)
        R_u = [cx.res(f"u{k}") for k in range(KC)]
        vecs = sb("vecs_sb", [128, nvec], F32)
        R_vecs = cx.res("vecs")
        mod = sb("mod", [128, DEPTH, 48, 5], F32)
        R_mod = cx.res("mod")
        gm = sb("gm", [128, DEPTH, 2, KC, 5], F32)
        R_gm = cx.res("gm")
        ones = sb("ones", [128, 128], F32)
        R_ones = cx.res("ones")
        onesb = sb("onesb", [128, 128], BF16)
        R_onesb = cx.res("onesb")
        rs = sb("rs", [128, 2, 512], F32)
        R_rs = [cx.res("rs0"), cx.res("rs1")]
        scr = sb("scr", [128, 4, 512], F32)
        R_scr = [cx.res(f"scr{i}") for i in range(4)]
        sv = sb("sv", [128, 64], F32)
        R_sv = cx.res("sv")
        NWB = 4
        wb = sb("wb", [128, NWB, 2048], BF16)
        R_wb = [cx.res(f"wb{i}") for i in range(NWB)]
        BIGE = 27776
        big = sb("big", [128, BIGE], BF16)
        R_big = [cx.res("big0"), cx.res("big1")]

        sel = sb("sel_sb", [128, 64, 128], BF16)
        R_sel = cx.res("sel")
        a8m = sb("a8m", [128, 2, 2, 64], F32)
        R_a8 = cx.res("a8")
        hst = sb("hst", [128, 3, 2, 32], F32)
        identb = sb("identb", [128, 128], BF16)
        R_identb = cx.res("identb")
        rcp = sb("rcp", [128, 4], F32)
        R_rcp = [cx.res(f"rcp{i}") for i in range(4)]
        R_hst = [cx.res("hs0"), cx.res("hs1"), cx.res("hs2")]
        R_s5w = cx.res("s5w")
        R_a16d = cx.res("a16d")
        pst = [es.enter_context(nc.psum_tensor(f"ps{i}", [128, 512], F32)) for i in range(8)]
        R_ps = [cx.res(f"ps{i}") for i in range(8)]

        def V(name, idx=0):
            off, n = vec_cols[name]
            return vecs[:, off + idx: off + idx + 1]

        cx.dma("sp", vecs[:], vecs_d[:, :], writes=[R_vecs])
        condt = sb("condt", [128, KC, 5], F32)
        R_cond = cx.res("cond")
        cx.dma("sp", condt[:].rearrange("p k j -> p (k j)"), cond[:, :], writes=[R_cond])
        cx.op("dve", lambda e: e.memset(ones[:], 1.0), writes=[R_ones])
        cx.op("dve", lambda e: e.memset(onesb[:], 1.0), writes=[R_onesb])
        cx.dma("pool", identb[:], cst_d[:, 0:128], writes=[R_identb])
        for k in range(KC):
            cx.op("pool", lambda e, k=k: e.memset(u[:, k, :], 0.0), writes=[R_u[k]])
        sct = sb("sct", [128, KC, 5], F32)
        R_sct = cx.res("sct")
        cx.op("act", lambda e: e.activation(out=sct[:], in_=condt[:], func=AF.Silu),
              reads=[R_cond], writes=[R_sct])

        APW = 384
        wa = big[:, 0:2 * KC * APW * 2].bitcast(F32).rearrange("p (a kc n) -> p a kc n", a=2, kc=KC)
        R_wa = R_big
        ada_v = ada_w.rearrange("l (kc p) n -> l p kc n", p=128)
        pieces = [(l, pc) for l in range(cfg.nl) for pc in range(6 * D // APW)]
        for i, (l, pc) in enumerate(pieces[:2]):
            cx.dma("act", wa[:, i % 2], ada_v[l, :, :, pc * APW:(pc + 1) * APW], writes=[R_wa[i % 2]])
        ada_todo = []

        def ada_piece(i, l, pc):
            bi = i % 2
            for m in range(APW // 128):
                idx = pc * (APW // 128) + m
                pb = idx % 2
                mms = [(pst[pb][:, 0:5], wa[:, bi, kc, m * 128:(m + 1) * 128], sct[:, kc, :])
                       for kc in range(KC)]
                cx.mm_group(mms, reads=[R_wa[bi], R_sct], writes=[R_ps[pb]])
                cx.op("act", lambda e, l=l, idx=idx, pb=pb: e.activation(
                    out=mod[:, l, idx, :], in_=pst[pb][:, 0:5], func=AF.Identity,
                    bias=V("adab", l * 48 + idx), scale=1.0),
                    reads=[R_ps[pb], R_vecs], writes=[R_mod])
            if i + 2 < len(pieces):
                l2, pc2 = pieces[i + 2]
                cx.dma("act", wa[:, bi], ada_v[l2, :, :, pc2 * APW:(pc2 + 1) * APW], writes=[R_wa[bi]])

        for i, (l, pc) in enumerate(pieces):
            ada_todo.append((i, l, pc))

        def ada_step(frac):
            ntodo = int(round(len(pieces) * frac))
            while ada_todo and ntodo > 0:
                ada_piece(*ada_todo.pop(0))
                ntodo -= 1

        def ada_finish():
            while ada_todo:
                ada_piece(*ada_todo.pop(0))
            ada_gm()

        def ada_gm():
          for l in range(cfg.nl):
            for which, (gname, scoff) in enumerate((("n1g", 8), ("n2g", 32))):
                goff = vec_cols[gname][0] + l * KC
                gb = vecs[:, goff:goff + KC].unsqueeze(2).to_broadcast([128, KC, 5])
                cx.op("dve", lambda e, l=l, which=which, scoff=scoff, gb=gb: e.scalar_tensor_tensor(
                    out=gm[:, l, which], in0=mod[:, l, scoff:scoff + KC, :], scalar=1.0, in1=gb,
                    op0=ALU.add, op1=ALU.mult),
                    reads=[R_mod, R_vecs], writes=[R_gm])

        def s5_gen():
            R_g = cx.res("s5gen")
            hs = h[:].rearrange("p k t -> p (k t)")
            st = {"off": 0}

            def carve(n):
                ap = hs[:, st["off"]:st["off"] + n]
                st["off"] += n
                return ap

            def tt(out, a, b_, op, eng="dve"):
                cx.op(eng, lambda e: e.tensor_tensor(out=out, in0=a, in1=b_, op=op), reads=[R_g, R_vecs], writes=[R_g])

            def ts(out, a, s1, op0, s2=None, op1=None):
                if op1 is None:
                    cx.op("dve", lambda e: e.tensor_scalar(out=out, in0=a, scalar1=s1, scalar2=None, op0=op0),
                          reads=[R_g], writes=[R_g])
                else:
                    cx.op("dve", lambda e: e.tensor_scalar(out=out, in0=a, scalar1=s1, scalar2=s2, op0=op0, op1=op1),
                          reads=[R_g], writes=[R_g])

            def act(out, a, func, scale=1.0, bias=None):
                if bias is None:
                    cx.op("act", lambda e: e.activation(out=out, in_=a, func=func, scale=scale), reads=[R_g], writes=[R_g])
                else:
                    cx.op("act", lambda e: e.activation(out=out, in_=a, func=func, scale=scale, bias=bias),
                          reads=[R_g, R_vecs], writes=[R_g])

            a3 = carve(192).rearrange("p (a g) -> p a g", a=3)
            cstt = carve(386)
            cx.dma("sp", a3.rearrange("p a g -> p (a g)"), s5a_d[:, :], writes=[R_g])
            cx.dma("sp", cstt, cst_d[:, :], writes=[R_g])
            cx.dma("pool", sel[:].rearrange("p a n -> p (a n)"), sel_d[:, :], writes=[R_sel], max_dma_last_dim=4096)
            ident = cstt[:, 0:128]
            mfr = cstt[:, 128:384]
            import os as _os3
            GD = int(_os3.environ.get("GENDBG", "9"))
            if GD < 1:
                return
            G = 64
            lr = carve(G); dt = carve(G); er = carve(G); th = carve(G)
            t1 = carve(G); t2 = carve(G); t3 = carve(G); mag = carve(G)
            pr = carve(9 * G).rearrange("p (k g) -> p k g", k=9)
            pi = carve(9 * G).rearrange("p (k g) -> p k g", k=9)
            ir = carve(G); ii = carve(G); qr = carve(G); qi = carve(G)
            ts(lr, a3[:, 0, :], -1e-4, ALU.min)
            act(dt, a3[:, 2, :], AF.Exp)
            tt(er, lr, dt, ALU.mult)
            tt(th, a3[:, 1, :], dt, ALU.mult)
            ts(th, th, 1.0 / (2.0 * np.pi), ALU.mult)
            MAGIC = 12582912.0

            def power(k, o_r, o_i):
                act(mag, er, AF.Exp, scale=float(k))
                for (dst, shift) in ((t3, 0.0), (t2, 0.25)):
                    ts(t1, th, float(k), ALU.mult, shift, ALU.add)
                    ts(dst, t1, MAGIC, ALU.add, MAGIC, ALU.subtract)
                    tt(t1, t1, dst, ALU.subtract)
                    act(dst, t1, AF.Sin, scale=2.0 * np.pi)
                tt(o_r, mag, t2, ALU.mult)
                tt(o_i, mag, t3, ALU.mult)

            if GD < 2:
                return
            cx.op("dve", lambda e: e.memset(pr[:, 0, :], 1.0), reads=[R_g], writes=[R_g])
            cx.op("dve", lambda e: e.memset(pi[:, 0, :], 0.0), reads=[R_g], writes=[R_g])
            for k in range(1, 9):
                power(k, pr[:, k, :], pi[:, k, :])
            power(-8, ir, ii)
            cx.op("dve", lambda e: e.tensor_copy(out=a8m[:, 0, 0, :], in_=pr[:, 8, :]), reads=[R_g], writes=[R_a8])
            cx.op("dve", lambda e: e.tensor_copy(out=a8m[:, 0, 1, :], in_=pr[:, 8, :]), reads=[R_g], writes=[R_a8])
            cx.op("dve", lambda e: e.tensor_copy(out=a8m[:, 1, 0, :], in_=pi[:, 8, :]), reads=[R_g], writes=[R_a8])
            cx.op("dve", lambda e: e.tensor_scalar(out=a8m[:, 1, 1, :], in0=pi[:, 8, :], scalar1=-1.0, scalar2=None,
                                                   op0=ALU.mult), reads=[R_g], writes=[R_a8])
            if GD < 3:
                return
            a16t = carve(4 * G).rearrange("p (c r g) -> p c r g", c=2, r=2)
            power(16, a16t[:, 0, 0, :], a16t[:, 1, 0, :])
            cx.op("dve", lambda e: e.tensor_copy(out=a16t[:, 0, 1, :], in_=a16t[:, 0, 0, :]), reads=[R_g], writes=[R_g])
            cx.op("dve", lambda e: e.tensor_scalar(out=a16t[:, 1, 1, :], in0=a16t[:, 1, 0, :], scalar1=-1.0, scalar2=None,
                                                   op0=ALU.mult), reads=[R_g], writes=[R_g])
            cx.dma("sp", a16_d[:, :], a16t.rearrange("p c r g -> p (c r g)"), reads=[R_g], writes=[R_a16d])
            am1 = carve(G); den = carve(G)
            li = a3[:, 1, :]
            ts(am1, pr[:, 1, :], -1.0, ALU.add)
            tt(t1, lr, lr, ALU.mult)
            tt(t2, li, li, ALU.mult)
            tt(den, t1, t2, ALU.add)
            cx.op("dve", lambda e: e.reciprocal(out=den, in_=den), reads=[R_g], writes=[R_g])
            tt(t1, am1, lr, ALU.mult)
            tt(t2, pi[:, 1, :], li, ALU.mult)
            tt(t1, t1, t2, ALU.add)
            tt(qr, t1, den, ALU.mult)
            tt(t1, pi[:, 1, :], lr, ALU.mult)
            tt(t2, am1, li, ALU.mult)
            tt(t1, t1, t2, ALU.subtract)
            tt(qi, t1, den, ALU.mult)

            GB = 16
            bcb = carve(4 * GB * 16).rearrange("p (a g c) -> p a g c", a=4, g=GB)
            Bbr = carve(GB * 16).rearrange("p (g c) -> p g c", g=GB)
            Bbi = carve(GB * 16).rearrange("p (g c) -> p g c", g=GB)
            u1 = carve(GB * 16).rearrange("p (g c) -> p g c", g=GB)
            u2 = carve(GB * 16).rearrange("p (g c) -> p g c", g=GB)
            PB = carve(2 * GB * 128).rearrange("p (r g s c) -> p r g s c", r=2, g=GB, s=8)
            Et = carve(2 * GB * 128).rearrange("p (r g n) -> p r g n", r=2, g=GB)
            Fm = carve(2 * GB * 128).rearrange("p (r g s c) -> p r g s c", r=2, g=GB, s=8)
            t12 = carve(256)
            Fz = carve(2 * 2 * 128).rearrange("p (h r n) -> p h r n", h=2, r=2)
            assert st["off"] <= KC * NTOK, st["off"]
            stage = big[:, 12288:12288 + GB * 896].rearrange("p (g n) -> p g n", g=GB)
            R_stage = cx.res("stage")

            def bc_g(ap2d):
                return ap2d.unsqueeze(2).to_broadcast([ap2d.shape[0], GB, 16])

            for blk in range(64 // GB if GD >= 4 else 0):
                gs = slice(blk * GB, (blk + 1) * GB)
                ada_step(0.2)
                cx.dma("sp", bcb, s5bc_d[:, :, gs, :], writes=[R_g])
                Bre, Bim, Cre, Cim = bcb[:, 0], bcb[:, 1], bcb[:, 2], bcb[:, 3]
                tt(u1, Bre, bc_g(qr[:, gs]), ALU.mult)
                tt(u2, Bim, bc_g(qi[:, gs]), ALU.mult)
                tt(Bbr, u1, u2, ALU.subtract)
                tt(u1, Bim, bc_g(qr[:, gs]), ALU.mult)
                tt(u2, Bre, bc_g(qi[:, gs]), ALU.mult)
                tt(Bbi, u1, u2, ALU.add)
                for s_ in range(8):
                    for half in range(2):
                        rows = slice(64 * half, 64 * half + 64)
                        k = (7 - s_) if half == 0 else s_
                        pkr = bc_g(pr[rows, k, gs]); pki = bc_g(pi[rows, k, gs])
                        tt(u1[rows], Bbr[rows], pkr, ALU.mult)
                        tt(u2[rows], Bbi[rows], pki, ALU.mult)
                        tt(PB[rows, 0, :, s_, :], u1[rows], u2[rows], ALU.subtract)
                        tt(u1[rows], Bbi[rows], pkr, ALU.mult)
                        tt(u2[rows], Bbr[rows], pki, ALU.mult)
                        tt(PB[rows, 1, :, s_, :], u1[rows], u2[rows], ALU.add)
                        k = (s_ + 1) if half == 0 else (8 - s_)
                        pkr = bc_g(pr[rows, k, gs]); pki = bc_g(pi[rows, k, gs])
                        tt(u1[rows], Cre[rows], pkr, ALU.mult)
                        tt(u2[rows], Cim[rows], pki, ALU.mult)
                        tt(Fm[rows, 0, :, s_, :], u1[rows], u2[rows], ALU.subtract)
                        tt(u1[rows], Cim[rows], pkr, ALU.mult)
                        tt(u2[rows], Cre[rows], pki, ALU.mult)
                        cx.op("dve", lambda e: e.scalar_tensor_tensor(
                            out=Fm[rows, 1, :, s_, :], in0=u1[rows], scalar=-1.0, in1=u2[rows],
                            op0=ALU.mult, op1=ALU.subtract), reads=[R_g], writes=[R_g])
                PBf = PB.rearrange("p r g s c -> p r g (s c)")
                Ff = Fm.rearrange("p r g s c -> p r g (s c)")

                def bc_n(ap2d):
                    return ap2d.unsqueeze(2).to_broadcast([128, GB, 128])
                tt(Et[:, 0], PBf[:, 0], bc_n(ir[:, gs]), ALU.mult)
                tt(Et[:, 1], PBf[:, 1], bc_n(ii[:, gs]), ALU.mult)
                tt(Et[:, 0], Et[:, 0], Et[:, 1], ALU.subtract)
                tt(Et[:, 1], PBf[:, 1], bc_n(ir[:, gs]), ALU.mult)
                for gg in range(GB):
                    tt(t12[:, 0:128], PBf[:, 0, gg, :], ii[:, blk * GB + gg: blk * GB + gg + 1].to_broadcast([128, 128]), ALU.mult)
                    tt(Et[:, 1, gg, :], Et[:, 1, gg, :], t12[:, 0:128], ALU.add)
                if GD < 5:
                    continue
                for dr in range(2):
                    for ri in range(2):
                        c0_ = 384 + dr * 256 + ri * 128
                        cx.op("dve", lambda e: e.tensor_scalar(
                            out=stage[:, :, c0_:c0_ + 128], in0=Ff[:, ri], scalar1=cstt[:, 384 + dr:385 + dr],
                            scalar2=None, op0=ALU.mult), reads=[R_g], writes=[R_stage])
                for gg in range(GB if GD >= 6 else 0):
                    pb = gg % 2
                    G6 = _os3.environ.get("G6", "tm")
                    if "t" in G6:
                        cx.pe_raw([lambda e, ri=ri: e.transpose(pst[pb][:, ri * 128:(ri + 1) * 128], PBf[:, ri, gg, :], ident)
                                   for ri in range(2)], reads=[R_g], writes=[R_ps[pb]])
                        cx.op("act", lambda e: e.activation(out=stage[:, gg, 0:256], in_=pst[pb][:, 0:256], func=AF.Identity),
                              reads=[R_ps[pb]], writes=[R_stage])
                    if "m" not in G6:
                        continue
                    pb2 = 2 + gg % 2
                    groups = []
                    for half in range(2):
                        cx.op("dve", lambda e: e.tensor_scalar(
                            out=Fz[:, half], in0=Ff[:, :, gg, :], scalar1=cstt[:, 384 + half:385 + half], scalar2=None,
                            op0=ALU.mult), reads=[R_g], writes=[R_g])
                        groups.append([(pst[pb2][:, half * 128:(half + 1) * 128], Et[:, ri, gg, :], Fz[:, half, ri, :])
                                       for ri in range(2)])
                    cx.mm_multi(groups, reads=[R_g], writes=[R_ps[pb2]])
                    cx.op("dve", lambda e: e.tensor_tensor(out=t12, in0=pst[pb2][:, 0:256], in1=mfr, op=ALU.mult),
                          reads=[R_ps[pb2], R_g], writes=[R_g])
                    cx.op("dve", lambda e: e.tensor_tensor(out=stage[:, gg, 256:384], in0=t12[:, 0:128], in1=t12[:, 128:256],
                                                           op=ALU.add), reads=[R_g], writes=[R_stage])
                if GD >= 7:
                    cx.dma("sp", s5w_d.rearrange("g p n -> p g n")[:, gs, :], stage, reads=[R_stage], writes=[R_s5w])

        import os as _os
        S5DBG = int(_os.environ.get("S5DBG", "9"))
        ada_step(0.15)
        if cfg.nl > 1 and S5DBG >= 0:
            s5_gen()
        ada_finish()
        cx.barrier()

        plan = []
        for b in range(NB):
            for l in range(cfg.nl):
                kind = l % 3
                if kind == 0:
                    plan.append(("pool", l // 3))
                elif kind == 1:
                    import os as _os2
                    _d = int(_os2.environ.get("S5DBG", "9"))
                    for ps_ in range(2 if _d >= 1 else 0):
                        for i in range(4):
                            plan.append(("s5wb", ps_, i))
                        for i in range(11 if _d >= 4 else 0):
                            plan.append(("s5we", ps_, i))
                    for m in range(KC if _d >= 6 else 0):
                        plan.append(("glu", m))
                elif kind == 2:
                    for hp in range(8):
                        plan.append(("nqk", hp))
                        plan.append(("nv", hp))
                        for hh in range(2):
                            plan.append(("nbA", 2 * hp + hh))
                            plan.append(("nbB", 2 * hp + hh))
                        plan.append(("nwo", hp))
                for g in range(4):
                    prs = FFN_GROUPS[g]
                    for j in prs:
                        plan.append(("wup", l, j))
                    for jj in range(0, len(prs), 2):
                        plan.append(("wdn", l, prs[jj:jj + 2]))
        ws_state = {"issued": 0, "next": 0}

        def ws_issue():
            i = ws_state["issued"]
            if i >= len(plan):
                return
            item = plan[i]
            bi = i % NWB
            kind = item[0]
            if kind == "wup":
                _, l, j = item
                cx.dma("pool", wb[:, bi, :], wup[l, j], writes=[R_wb[bi]], max_dma_last_dim=4096)
            elif kind == "wdn":
                _, l, js = item
                src = wdn[l].rearrange("(fc p) d -> p fc d", p=128)[:, js[0]:js[0] + len(js), :]
                dst = wb[:, bi, 0:len(js) * 1024].rearrange("p (a d) -> p a d", d=1024)
                cx.dma("pool", dst, src, writes=[R_wb[bi]], max_dma_last_dim=4096)
            elif kind == "s5wb":
                _, ps_, i = item
                g0 = ps_ * 32 + i * 8
                src = s5w_d[g0:g0 + 8, :, 0:256].rearrange("g p n -> p g n")
                dst = wb[:, bi, 0:2048].rearrange("p (g n) -> p g n", g=8)
                cx.dma("pool", dst, src, reads=[R_s5w], writes=[R_wb[bi]])
            elif kind == "s5we":
                _, ps_, i = item
                g0 = ps_ * 32 + i * 3
                ng = min(3, ps_ * 32 + 32 - g0)
                src = s5w_d[g0:g0 + ng, :, 256:896].rearrange("g p n -> p g n")
                dst = wb[:, bi, 0:ng * 640].rearrange("p (g n) -> p g n", g=ng)
                cx.dma("pool", dst, src, reads=[R_s5w], writes=[R_wb[bi]])
            elif kind == "nqk":
                _, hp = item
                for wi in range(2):
                    src = nqkv_d.rearrange("(kc p) n -> p kc n", p=128)[:, :, wi * D + hp * 128: wi * D + (hp + 1) * 128]
                    dst = wb[:, bi, wi * 1024:(wi + 1) * 1024].rearrange("p (kc n) -> p kc n", kc=KC)
                    cx.dma("pool", dst, src, writes=[R_wb[bi]])
            elif kind == "nv":
                _, hp = item
                src = nqkv_d.rearrange("(kc p) n -> p kc n", p=128)[:, :, 2 * D + hp * 128: 2 * D + (hp + 1) * 128]
                dst = wb[:, bi, 0:1024].rearrange("p (kc n) -> p kc n", kc=KC)
                cx.dma("pool", dst, src, writes=[R_wb[bi]])
            elif kind == "nbA":
                _, hd = item
                cx.dma("pool", wb[:, bi, 0:1152].rearrange("p (t n) -> p t n", t=9), btab_d[hd, :, 0:9, :],
                       writes=[R_wb[bi]], max_dma_last_dim=4096)
            elif kind == "nbB":
                _, hd = item
                cx.dma("pool", wb[:, bi, 0:1536].rearrange("p (t n) -> p t n", t=12), btab_d[hd, :, 9:21, :],
                       writes=[R_wb[bi]], max_dma_last_dim=4096)
            elif kind == "nwo":
                _, hp = item
                cx.dma("pool", wb[:, bi, 0:1024], nwo_d[hp * 128:(hp + 1) * 128, :], writes=[R_wb[bi]])
            elif kind == "glu":
                _, m = item
                for wi, wsrc in enumerate((s5w1_d, s5w2_d)):
                    src = wsrc.rearrange("(kc p) n -> p kc n", p=128)[:, :, m * 128:(m + 1) * 128]
                    dst = wb[:, bi, wi * 1024:(wi + 1) * 1024].rearrange("p (kc n) -> p kc n", kc=KC)
                    cx.dma("pool", dst, src, writes=[R_wb[bi]])
            elif kind == "pool":
                _, s = item
                src = pool_w[s].rearrange("g (kk p) n -> p g kk n", p=128)
                dst = wb[:, bi, :].rearrange("p (g kk n) -> p g kk n", g=4, kk=2)
                cx.dma("pool", dst, src, writes=[R_wb[bi]])
            ws_state["issued"] += 1

        def ws_next(expect):
            i = ws_state["next"]
            assert plan[i][0] == expect[0] and tuple(plan[i][1:]) == tuple(expect[1:]), (plan[i], expect)
            while ws_state["issued"] <= i:
                ws_issue()
            ws_state["next"] += 1
            return i % NWB

        def ws_done():
            while ws_state["issued"] < min(len(plan), ws_state["next"] + NWB - 1):
                ws_issue()

        for _ in range(NWB - 1):
            ws_issue()

        TILES512 = [(t0, min(512, NTOK - t0)) for t0 in range(0, NTOK, 512)]

        def ucol(t):
            return LAT0 + t if t < SEQ else CTX0 + (t - SEQ)

        def norm_phase(l, which, b, ctx_too, out_bf16=True):
            shoff = 0 if which == 0 else 24
            tiles = [t for t in TILES512 if ctx_too or t[0] < SEQ]
            cx.barrier()
            sqb = scr[:, 0:2].rearrange("p a n -> p (a n)").bitcast(BF16).rearrange("p (a n) -> p a n", a=4)
            R_sq = [cx.res(f"sq{i}") for i in range(4)]
            R_tm = [R_scr[2], R_scr[3]]
            cnt = {"sq": 0, "tm": 0}

            def stage_a(ti, t0, n):
                pb = 2 + ti % 2
                ri = ti % 2
                for k in range(KC):
                    si = cnt["sq"] % 4
                    cnt["sq"] += 1
                    cx.op("act", lambda e: e.activation(out=sqb[:, si, 0:n], in_=h[:, k, t0:t0 + n], func=AF.Square),
                          reads=[R_h[k]], writes=[R_sq[si]])
                    e = cx.E["pe"]
                    cx._deps(e, [R_sq[si], R_onesb], [R_ps[pb]] if k == 0 else [])
                    inst = e.h.matmul(pst[pb][:, 0:n], onesb[:], sqb[:, si, 0:n], start=(k == 0), stop=(k == KC - 1))
                    e.count += 1
                    inst.then_inc(e.sem, 1)
                    cx._register((e.sem, e.count), [R_sq[si], R_onesb], [R_ps[pb]])
                cx.op("act", lambda e: e.activation(out=rs[:, ri, 0:n], in_=pst[pb][:, 0:n], func=AF.Ln,
                                                    bias=V("epsv"), scale=1.0 / D),
                      reads=[R_ps[pb], R_vecs], writes=[R_rs[ri]])
                cx.op("act", lambda e: e.activation(out=rs[:, ri, 0:n], in_=rs[:, ri, 0:n], func=AF.Exp, scale=-0.5),
                      reads=[R_rs[ri]], writes=[R_rs[ri]])

            def stage_b(ti, t0, n):
                j = b if t0 < SEQ else 4
                ri = ti % 2
                c0 = ucol(t0)
                for k in range(KC):
                    si = 2 + cnt["tm"] % 2
                    cnt["tm"] += 1
                    cx.op("dve", lambda e: e.scalar_tensor_tensor(
                        out=scr[:, si, 0:n], in0=h[:, k, t0:t0 + n], scalar=gm[:, l, which, k, j:j + 1],
                        in1=rs[:, ri, 0:n], op0=ALU.mult, op1=ALU.mult),
                        reads=[R_h[k], R_gm, R_rs[ri]], writes=[R_scr[si]])
                    if False:
                        pass
                    else:
                        cx.op("pool", lambda e: e.tensor_scalar(
                            out=u[:, k, c0:c0 + n], in0=scr[:, si, 0:n], scalar1=1.0,
                            scalar2=mod[:, l, shoff + k, j:j + 1], op0=ALU.mult, op1=ALU.add),
                            reads=[R_scr[si], R_mod], writes=[R_u[k]])

            stage_a(0, *tiles[0])
            for ti in range(len(tiles)):
                if ti + 1 < len(tiles):
                    stage_a(ti + 1, *tiles[ti + 1])
                stage_b(ti, *tiles[ti])
            cx.barrier()

        def pool_phase(l, b, ctx_too):
            s = l // 3
            bi = ws_next(("pool", s))
            pw = wb[:, bi, :].rearrange("p (g kk n) -> p g kk n", g=4, kk=2)
            psoff = vec_cols["ps"][0] + s * KC
            pboff = vec_cols["pb"][0] + s * KC
            for ji, j in enumerate((b, 4)):
                cx.op("dve", lambda e, ji=ji, j=j: e.tensor_tensor(
                    out=sv[:, ji * 16:ji * 16 + 8], in0=vecs[:, psoff:psoff + KC], in1=mod[:, l, 16:24, j],
                    op=ALU.mult), reads=[R_vecs, R_mod], writes=[R_sv])
                cx.op("dve", lambda e, ji=ji: e.tensor_tensor(
                    out=sv[:, ji * 16 + 8:ji * 16 + 16], in0=vecs[:, pboff:pboff + KC],
                    in1=sv[:, ji * 16:ji * 16 + 8], op=ALU.mult), reads=[R_vecs, R_sv], writes=[R_sv])
            abuf = big[:, 0:2 * 2 * UCOLS].bitcast(F32).rearrange("p (a c) -> p a c", a=2)
            seqs = [(LAT0, SEQ, 0)] + ([(CTX0, CTX, SEQ)] if ctx_too else [])
            toff = vec_cols["ptbl"][0]
            for k in range(KC):
                wi = k // 2
                w = POOL_WINDOWS[wi]
                src = u[:, k, :]
                width = UCOLS
                step = 1
                ai = 0
                first = True
                while step < w:
                    width -= step
                    dst = abuf[:, ai, 0:width]
                    s0 = src[:, 0:width]
                    s1 = src[:, step:step + width]
                    cx.op("dve", lambda e, dst=dst, s0=s0, s1=s1: e.tensor_tensor(out=dst, in0=s0, in1=s1, op=ALU.add),
                          reads=[R_u[k], R_big[1]], writes=[R_big[1]])
                    src = abuf[:, ai, :]
                    ai ^= 1
                    step *= 2
                    first = False
                aw = src
                for (base, n, tb) in seqs:
                    cx.op("dve", lambda e, base=base, n=n, tb=tb, aw=aw: e.scalar_tensor_tensor(
                        out=u[:, k, base + w // 2:base + n - (w // 2 - 1)],
                        in0=aw[:, base: base + n - w + 1], scalar=1.0 / w,
                        in1=u[:, k, base + w // 2:base + n - (w // 2 - 1)], op0=ALU.mult, op1=ALU.subtract),
                        reads=[R_big[1], R_u[k]], writes=[R_u[k]])
                    nl_ = w // 2
                    tl = toff + wi * 16
                    cx.op("dve", lambda e, base=base, aw=aw, nl_=nl_, tl=tl: e.tensor_tensor(
                        out=scr[:, 0, 0:nl_], in0=aw[:, base - w // 2: base - w // 2 + nl_],
                        in1=vecs[:, tl:tl + nl_], op=ALU.mult), reads=[R_big[1], R_vecs], writes=[R_scr[0]])
                    cx.op("dve", lambda e, base=base, nl_=nl_, tb=tb: e.tensor_tensor(
                        out=u[:, k, base:base + nl_], in0=scr[:, 0, 0:nl_], in1=u[:, k, base:base + nl_],
                        op=ALU.subtract), reads=[R_scr[0], R_u[k]], writes=[R_u[k]])
                    nr = w // 2 - 1
                    if nr > 0:
                        t1 = n - w // 2 + 1
                        cx.op("dve", lambda e, base=base, aw=aw, nr=nr, tl=tl, t1=t1: e.tensor_tensor(
                            out=scr[:, 0, 0:nr], in0=aw[:, base + t1 - w // 2: base + t1 - w // 2 + nr],
                            in1=vecs[:, tl + 8:tl + 8 + nr], op=ALU.mult),
                            reads=[R_big[1], R_vecs], writes=[R_scr[0]])
                        cx.op("dve", lambda e, base=base, nr=nr, tb=tb, t1=t1: e.tensor_tensor(
                            out=u[:, k, base + t1:base + t1 + nr], in0=scr[:, 0, 0:nr],
                            in1=u[:, k, base + t1:base + t1 + nr], op=ALU.subtract),
                            reads=[R_scr[0], R_u[k]], writes=[R_u[k]])
            tiles = [t for t in TILES512 if ctx_too or t[0] < SEQ]
            cnt = 0
            for gi in range(4):
                for mi in range(2):
                    m = 2 * gi + mi
                    for (t0, n) in tiles:
                        ji = 0 if t0 < SEQ else 1
                        pb = 2 + cnt % 2
                        si = cnt % 2
                        cnt += 1
                        c0 = ucol(t0)
                        mms = [(pst[pb][:, 0:n], pw[:, gi, kk, mi * 128:(mi + 1) * 128], u[:, 2 * gi + kk, c0:c0 + n])
                               for kk in range(2)]
                        cx.mm_group(mms, reads=[R_wb[bi], R_u[2 * gi], R_u[2 * gi + 1]], writes=[R_ps[pb]])
                        cx.op("act", lambda e, pb=pb, si=si, m=m, ji=ji, n=n: e.activation(
                            out=scr[:, si, 0:n], in_=pst[pb][:, 0:n], func=AF.Identity,
                            bias=sv[:, ji * 16 + 8 + m: ji * 16 + 9 + m], scale=sv[:, ji * 16 + m: ji * 16 + m + 1]),
                            reads=[R_ps[pb], R_sv], writes=[R_scr[si]])
                        cx.op("dve", lambda e, si=si, m=m, t0=t0, n=n: e.tensor_tensor(
                            out=h[:, m, t0:t0 + n], in0=h[:, m, t0:t0 + n], in1=scr[:, si, 0:n], op=ALU.add),
                            reads=[R_scr[si], R_h[m]], writes=[R_h[m]])
            ws_done()

        def ffn_phase(l, b, ctx_too):
            cwoff = vec_cols["cw"][0] + l * 3 * 44
            cboff = vec_cols["cb"][0] + l * 44
            up_tiles = []
            for i in range(5):
                t0 = 410 * i
                n = min(410, SEQ - t0)
                up_tiles.append((LAT0 + t0 - 1, n, t0))
            if ctx_too:
                up_tiles.append((CTX0 - 1, CTX, SEQ))
            dn_tiles = [t for t in TILES512 if ctx_too or t[0] < SEQ]
            ecnt = 0
            dcnt = 0
            pend = []
            cx.barrier()
            NSET = 4
            ftmp = big[:, 6 * NTOK:6 * NTOK + NSET * 2 * 416 * 2].bitcast(F32).rearrange("p (a n) -> p a n", a=NSET * 2)
            R_ft = [cx.res(f"ft{i}") for i in range(NSET * 2)]
            for g in range(4):
                prs = FFN_GROUPS[g]
                hb = 0
                hid = big[:, 0:6 * NTOK].rearrange("p (a t) -> p a t", a=6)
                for jj, j in enumerate(prs):
                    bi = ws_next(("wup", l, j))
                    wt = wb[:, bi, :].rearrange("p (kc n) -> p kc n", kc=KC)
                    for (c0, n, t0) in up_tiles:
                        nin = n + 2
                        pv = 2 + 2 * (ecnt % 3)
                        pg = pv + 1
                        sV = 2 * (ecnt % NSET)
                        sG = sV + 1
                        ecnt += 1
                        for half, pbk in ((0, pv), (1, pg)):
                            mms = [(pst[pbk][:, 0:nin], wt[:, kc, half * 128:(half + 1) * 128], u[:, kc, c0:c0 + nin])
                                   for kc in range(KC)]
                            cx.mm_group(mms, reads=[R_wb[bi]] + R_u, writes=[R_ps[pbk]])
                        for half, pbk, si in ((0, pv, sV), (1, pg, sG)):
                            ch = half * NFC + j
                            w0 = vecs[:, cwoff + 0 * 44 + ch: cwoff + 0 * 44 + ch + 1]
                            w1 = vecs[:, cwoff + 1 * 44 + ch: cwoff + 1 * 44 + ch + 1]
                            w2 = vecs[:, cwoff + 2 * 44 + ch: cwoff + 2 * 44 + ch + 1]
                            bb = vecs[:, cboff + ch: cboff + ch + 1]
                            cx.op("act", lambda e, pbk=pbk, si=si, w1=w1, bb=bb, n=n: e.activation(
                                out=ftmp[:, si, 0:n], in_=pst[pbk][:, 1:1 + n], func=AF.Identity, bias=bb, scale=w1),
                                reads=[R_ps[pbk], R_vecs], writes=[R_ft[si]])
                            cx.op("dve", lambda e, pbk=pbk, si=si, w0=w0, n=n: e.scalar_tensor_tensor(
                                out=ftmp[:, si, 0:n], in0=pst[pbk][:, 0:n], scalar=w0, in1=ftmp[:, si, 0:n],
                                op0=ALU.mult, op1=ALU.add), reads=[R_ps[pbk], R_vecs, R_ft[si]], writes=[R_ft[si]])
                            cx.op("dve", lambda e, pbk=pbk, si=si, w2=w2, n=n: e.scalar_tensor_tensor(
                                out=ftmp[:, si, 0:n], in0=pst[pbk][:, 2:2 + n], scalar=w2, in1=ftmp[:, si, 0:n],
                                op0=ALU.mult, op1=ALU.add), reads=[R_ps[pbk], R_vecs, R_ft[si]], writes=[R_ft[si]])
                        def tail(n=n, jj=jj, t0=t0, sG=sG, sV=sV):
                            cx.op("act", lambda e: e.activation(out=ftmp[:, sG, 0:n], in_=ftmp[:, sG, 0:n], func=AF.Silu),
                                  reads=[R_ft[sG]], writes=[R_ft[sG]])
                            cx.op("pool", lambda e: e.tensor_tensor(
                                out=hid[:, jj, t0:t0 + n], in0=ftmp[:, sG, 0:n], in1=ftmp[:, sV, 0:n], op=ALU.mult),
                                reads=[R_ft[sG], R_ft[sV]], writes=[R_big[hb]])
                        if pend:
                            pend.pop()()
                        pend.append(tail)
                    ws_done()
                if pend:
                    pend.pop()()
                wts = []
                for jj in range(0, len(prs), 2):
                    bi = ws_next(("wdn", l, prs[jj:jj + 2]))
                    wts.append(bi)
                for c in range(KC):
                    for (t0, n) in dn_tiles:
                        j = b if t0 < SEQ else 4
                        pb = dcnt % 2
                        dcnt += 1
                        mms = []
                        for jj in range(len(prs)):
                            bi = wts[jj // 2]
                            wt = wb[:, bi, :].rearrange("p (a d) -> p a d", a=2)
                            mms.append((pst[pb][:, 0:n], wt[:, jj % 2, c * 128:(c + 1) * 128], hid[:, jj, t0:t0 + n]))
                        cx.mm_group(mms, reads=[R_wb[x] for x in wts] + [R_big[hb]], writes=[R_ps[pb]])
                        cx.op("dve", lambda e, pb=pb, c=c, t0=t0, n=n, j=j: e.scalar_tensor_tensor(
                            out=h[:, c, t0:t0 + n], in0=pst[pb][:, 0:n], scalar=mod[:, l, 40 + c, j:j + 1],
                            in1=h[:, c, t0:t0 + n], op0=ALU.mult, op1=ALU.add),
                            reads=[R_ps[pb], R_mod, R_h[c]], writes=[R_h[c]])
                ws_done()
            cx.barrier()

        prot = {"i": 0}

        def rot():
            prot["i"] = (prot["i"] + 1) % 8
            return prot["i"]

        def s5_phase(l, b):
            GP = 32
            Xb = big[:, 0:GP * 288].rearrange("p (g j) -> p g j", g=GP)
            Sh = big[:, GP * 288:GP * 288 + 2 * GP * 289].rearrange("p (r g j) -> p r g j", r=2, g=GP)
            R_X = R_big[0]
            R_S = R_big[1]
            Hs, T1, T2 = hst[:, 0], hst[:, 1], hst[:, 2]
            doff = vec_cols["s5d"][0]
            for ps_ in range(2 if S5DBG >= 1 else 0):
                cx.op("pool", lambda e: e.memset(Sh[:, :, :, 0:1], 0.0), writes=[R_S])
                cx.op("dve", lambda e: e.memset(Hs, 0.0), writes=[R_hst[0]])
                for gi in range(GP):
                    g = ps_ * GP + gi
                    kc, gl = g // 8, g % 8
                    q, par = gl // 2, gl % 2
                    rows, so = (slice(32 * q, 32 * q + 32), 0) if q < 3 else (slice(64, 128), 32)
                    pb = rot()
                    groups = [
                        [(pst[pb][:, 0:256], sel[rows, so + par * 8 + s_, :], u[rows, kc, LAT0 + s_:LAT0 + s_ + SEQ:8])
                         for s_ in range(8)],
                        [(pst[pb][:, 256:288], sel[rows, so + par * 8 + s_, :], u[rows, kc, CTX0 + s_:CTX0 + s_ + CTX:8])
                         for s_ in range(8)],
                    ]
                    cx.mm_multi(groups, reads=[R_sel, R_u[kc]], writes=[R_ps[pb]])
                    cx.op("act", lambda e: e.activation(out=Xb[:, gi, :], in_=pst[pb][:, 0:288], func=AF.Identity),
                          reads=[R_ps[pb]], writes=[R_X])
                for gi in range(GP):
                    if gi % 8 == 0:
                        if gi > 0:
                            ws_done()
                        bi = ws_next(("s5wb", ps_, gi // 8))
                        wv = wb[:, bi, 0:2048].rearrange("p (a n) -> p a n", a=8)
                    W = wv[:, gi % 8]
                    for ri in range(2 if S5DBG >= 2 else 0):
                        pb2 = rot()
                        wf = W[:, ri * 128:ri * 128 + 64]
                        wr = W[:, ri * 128 + 64:ri * 128 + 128]
                        groups = [
                            [(pst[pb2][0:64, 0:32], wf, Xb[:, gi, 256:288])],
                            [(pst[pb2][0:64, 32:288], wf, Xb[:, gi, 0:256])],
                            [(pst[pb2][64:128, 0:288], wr, Xb[:, gi, ::-1])],
                        ]
                        cx.mm_multi(groups, reads=[R_wb[bi], R_X], writes=[R_ps[pb2]])
                        cx.op("dve", lambda e: e.tensor_copy(out=Sh[:, ri, gi, 1:289], in_=pst[pb2][:, 0:288]),
                              reads=[R_ps[pb2]], writes=[R_S])
                ws_done()
                gsl = slice(ps_ * GP, (ps_ + 1) * GP)
                a16 = rs[:].rearrange("p a n -> p (a n)")[:, 0:256].rearrange("p (c r g) -> p c r g", c=2, r=2)
                if ps_ == 0:
                    cx.dma("sp", rs[:].rearrange("p a n -> p (a n)")[:, 0:256], a16_d[:, :], reads=[R_a16d],
                           writes=[R_rs[0]])
                mtab2 = a16[:, :, :, gsl]
                Pm = hst[:, 1:3]
                hsb = Hs.unsqueeze(1).to_broadcast([128, 2, 2, GP])
                scrf = scr[:].rearrange("p a n -> p (a n)")

                def pair_fix(dst_lo, src_lo):
                    for gl0 in range(0, GP, 3):
                        ng = min(3, GP - gl0)
                        T1 = scrf[:, 0:2 * ng * 144].rearrange("p (r g m) -> p r g m", r=2, g=ng)
                        T2 = scrf[:, 1024:1024 + 2 * ng * 144].rearrange("p (r g m) -> p r g m", r=2, g=ng)
                        src = Sh[:, :, gl0:gl0 + ng, src_lo:src_lo + 287:2]
                        dst = Sh[:, :, gl0:gl0 + ng, dst_lo:dst_lo + 287:2]
                        g0 = ps_ * GP + gl0
                        ar_ = a8m[:, 0, :, g0:g0 + ng].unsqueeze(3).to_broadcast([128, 2, ng, 144])
                        ai_ = a8m[:, 1, ::-1, g0:g0 + ng].unsqueeze(3).to_broadcast([128, 2, ng, 144])
                        cx.op("dve", lambda e: e.tensor_tensor(out=T1, in0=src, in1=ar_, op=ALU.mult),
                              reads=[R_S, R_a8], writes=[R_scr[0], R_scr[1]])
                        cx.op("pool", lambda e: e.tensor_tensor(out=T2, in0=src[:, ::-1], in1=ai_, op=ALU.mult),
                              reads=[R_S, R_a8], writes=[R_scr[2], R_scr[3]])
                        cx.op("dve", lambda e: e.tensor_tensor(out=T1, in0=T1, in1=T2, op=ALU.add),
                              reads=[R_scr[0], R_scr[1], R_scr[2], R_scr[3]], writes=[R_scr[0], R_scr[1]])
                        cx.op("dve", lambda e: e.tensor_tensor(out=dst, in0=dst, in1=T1, op=ALU.add),
                              reads=[R_scr[0], R_scr[1], R_S], writes=[R_S])

                if S5DBG >= 3:
                    pair_fix(2, 1)
                    for m in range(144):
                        col = 2 * m + 2
                        cx.op("dve", lambda e: e.tensor_tensor(out=Pm, in0=hsb, in1=mtab2, op=ALU.mult),
                              reads=[R_hst[0], R_rs[0]], writes=[R_hst[1]])
                        cx.op("dve", lambda e: e.tensor_tensor(out=Pm[:, 0], in0=Pm[:, 0], in1=Pm[:, 1, ::-1, :], op=ALU.add),
                              reads=[R_hst[1]], writes=[R_hst[1]])
                        cx.op("dve", lambda e: e.tensor_tensor(out=Hs, in0=Pm[:, 0], in1=Sh[:, :, :, col], op=ALU.add),
                              reads=[R_hst[1], R_S], writes=[R_hst[0]])
                        cx.op("act", lambda e: e.activation(out=Sh[:, :, :, col], in_=Hs, func=AF.Identity),
                              reads=[R_hst[0]], writes=[R_S])
                    pair_fix(1, 0)
                for gi in range(GP if S5DBG >= 4 else 0):
                    if gi % 3 == 0:
                        if gi > 0:
                            ws_done()
                        bi = ws_next(("s5we", ps_, gi // 3))
                        wv = wb[:, bi, 0:1920].rearrange("p (a n) -> p a n", a=3)
                    W = wv[:, gi % 3]
                    pb = rot()
                    mms = [(pst[pb][:, 0:288], W[:, 0:128], Xb[:, gi, 0:288])]
                    for ri in range(2):
                        wcf = W[:, 128 + ri * 128:128 + (ri + 1) * 128]
                        mms.append((pst[pb][:, 256:288], wcf, Sh[:, ri, gi, 0:32]))
                        mms.append((pst[pb][:, 0:256], wcf, Sh[:, ri, gi, 32:288]))
                    for ri in range(2):
                        wcr = W[:, 384 + ri * 128:384 + (ri + 1) * 128]
                        mms.append((pst[pb][:, 0:288], wcr, Sh[:, ri, gi, 287::-1]))
                    cx.mm_group(mms, reads=[R_wb[bi], R_X, R_S], writes=[R_ps[pb]])
                    cx.op("act", lambda e: e.activation(out=Xb[:, gi, :], in_=pst[pb][:, 0:288], func=AF.Identity),
                          reads=[R_ps[pb]], writes=[R_X])
                ws_done()
                for kcl in range(4 if S5DBG >= 5 else 0):
                    kc = ps_ * 4 + kcl
                    for s_ in range(8):
                        q, par = s_ // 2, s_ % 2
                        rows, so = (slice(32 * q, 32 * q + 32), 0) if q < 3 else (slice(64, 128), 32)
                        pb = rot()
                        mms = [(pst[pb][:, 0:288], sel[rows, so + 16 + gl * 2 + par, :], Xb[rows, kcl * 8 + gl, 0:288])
                               for gl in range(8)]
                        cx.mm_group(mms, reads=[R_sel, R_X], writes=[R_ps[pb]])
                        for (c0, n, p0) in ((LAT0 + s_, SEQ // 8, 0), (CTX0 + s_, CTX // 8, 256)):
                            si = rot() % 4
                            uv = u[:, kc, c0:c0 + 8 * n:8]
                            cx.op("dve", lambda e: e.scalar_tensor_tensor(
                                out=scr[:, si, 0:n], in0=uv, scalar=vecs[:, doff + kc:doff + kc + 1],
                                in1=pst[pb][:, p0:p0 + n], op0=ALU.mult, op1=ALU.add),
                                reads=[R_u[kc], R_vecs, R_ps[pb]], writes=[R_scr[si]])
                            cx.op("act", lambda e: e.activation(out=uv, in_=scr[:, si, 0:n], func=AF.Gelu_apprx_tanh),
                                  reads=[R_scr[si]], writes=[R_u[kc]])
            b1off = vec_cols["s5b1"][0]
            b2off = vec_cols["s5b2"][0]
            for m in range(KC if S5DBG >= 6 else 0):
                bi = ws_next(("glu", m))
                wv = wb[:, bi, :].rearrange("p (w kc n) -> p w kc n", w=2, kc=KC)
                for (t0, n) in TILES512:
                    j = b if t0 < SEQ else 4
                    c0 = ucol(t0)
                    pa, pbk = rot(), rot()
                    for wi, pk in ((0, pa), (1, pbk)):
                        mms = [(pst[pk][:, 0:n], wv[:, wi, kc, :], u[:, kc, c0:c0 + n]) for kc in range(KC)]
                        cx.mm_group(mms, reads=[R_wb[bi]] + R_u, writes=[R_ps[pk]])
                    si = rot() % 4
                    sj = (si + 1) % 4
                    cx.op("act", lambda e: e.activation(out=scr[:, si, 0:n], in_=pst[pbk][:, 0:n], func=AF.Sigmoid,
                                                        bias=vecs[:, b2off + m:b2off + m + 1], scale=1.0),
                          reads=[R_ps[pbk], R_vecs], writes=[R_scr[si]])
                    cx.op("dve", lambda e: e.scalar_tensor_tensor(
                        out=scr[:, sj, 0:n], in0=pst[pa][:, 0:n], scalar=vecs[:, b1off + m:b1off + m + 1],
                        in1=scr[:, si, 0:n], op0=ALU.add, op1=ALU.mult),
                        reads=[R_ps[pa], R_vecs, R_scr[si]], writes=[R_scr[sj]])
                    cx.op("dve", lambda e: e.scalar_tensor_tensor(
                        out=h[:, m, t0:t0 + n], in0=scr[:, sj, 0:n], scalar=mod[:, l, 16 + m, j:j + 1],
                        in1=h[:, m, t0:t0 + n], op0=ALU.mult, op1=ALU.add),
                        reads=[R_scr[sj], R_mod, R_h[m]], writes=[R_h[m]])
                ws_done()

        def nat_tiles(qb):
            if 2 <= qb <= 13:
                return [(qb - 2 + i, i) for i in range(5)]
            base = {0: 5, 1: 9, 14: 13, 15: 17}[qb]
            k0 = 0 if qb < 2 else 12
            return [(k0 + i, base + i) for i in range(4)]

        def nat_phase(l, b):
            cx.barrier()
            pm0 = V("pm0")
            pm1 = V("pm1")
            NQ = 2048 + 2 * 2304 + 18 * 2 * 65
            R_qkv = [cx.res("nqkv0"), cx.res("nqkv1")]
            R_ot = cx.res("notok")
            R_oth = [cx.res("noth0"), cx.res("noth1")]
            R_pt = [cx.res("npt0"), cx.res("npt1")]
            o0 = 2 * NQ
            OTK = big[:, o0:o0 + 2048].rearrange("p (q h d) -> p q h d", q=16, h=2)
            OTH = big[:, o0 + 2048:o0 + 2048 + 4096].rearrange("p (a t) -> p a t", a=2)
            PT = big[:, o0 + 6144:o0 + 6144 + 1792].rearrange("p (a t n) -> p a t n", a=2, t=7)
            assert o0 + 6144 + 1792 <= BIGE
            pcnt = 0
            npend = []
            for hp in range(8):
                qi = hp % 2
                QT = big[:, qi * NQ:qi * NQ + 2048]
                KZ = big[:, qi * NQ + 2048:qi * NQ + 2048 + 4608].rearrange("p (z t) -> p z t", z=2)
                VA = big[:, qi * NQ + 6656:qi * NQ + 6656 + 2340].rearrange("p (t h d) -> p t h d", t=18, h=2)
                RQ = R_qkv[qi]
                cx.op("pool", lambda e: e.memset(VA[:, :, :, 64:65], 1.0), writes=[RQ])
                bi = ws_next(("nqk", hp))
                wqk = wb[:, bi, :].rearrange("p (w kc n) -> p w kc n", w=2, kc=KC)
                for (t0, n) in TILES512:
                    c0 = ucol(t0)
                    if t0 < SEQ:
                        pb = rot()
                        cx.mm_group([(pst[pb][:, 0:n], wqk[:, 0, kc, :], u[:, kc, c0:c0 + n]) for kc in range(KC)],
                                    reads=[R_wb[bi]] + R_u, writes=[R_ps[pb]])
                        cx.op("act", lambda e: e.activation(out=QT[:, t0:t0 + n], in_=pst[pb][:, 0:n], func=AF.Identity,
                                                            scale=0.125), reads=[R_ps[pb]], writes=[RQ])
                    pb = rot()
                    cx.mm_group([(pst[pb][:, 0:n], wqk[:, 1, kc, :], u[:, kc, c0:c0 + n]) for kc in range(KC)],
                                reads=[R_wb[bi]] + R_u, writes=[R_ps[pb]])
                    cx.op("act", lambda e: e.activation(out=KZ[:, 0, t0:t0 + n], in_=pst[pb][:, 0:n], func=AF.Identity,
                                                        scale=pm0), reads=[R_ps[pb], R_vecs], writes=[RQ])
                    cx.op("dve", lambda e: e.tensor_scalar(out=KZ[:, 1, t0:t0 + n], in0=pst[pb][:, 0:n], scalar1=pm1,
                                                           scalar2=None, op0=ALU.mult),
                          reads=[R_ps[pb], R_vecs], writes=[RQ])
                ws_done()
                bi = ws_next(("nv", hp))
                wv_ = wb[:, bi, 0:1024].rearrange("p (kc n) -> p kc n", kc=KC)
                for t4 in range(0, 18, 4):
                    nt = min(4, 18 - t4)
                    pb = rot()
                    groups = []
                    for tt in range(t4, t4 + nt):
                        c0 = ucol(128 * tt)
                        groups.append([(pst[pb][:, (tt - t4) * 128:(tt - t4 + 1) * 128], u[:, kc, c0:c0 + 128], wv_[:, kc, :])
                                       for kc in range(KC)])
                    cx.mm_multi(groups, reads=[R_wb[bi]] + R_u, writes=[R_ps[pb]])
                    cx.op("dve", lambda e: e.tensor_copy(
                        out=VA[:, t4:t4 + nt, :, 0:64],
                        in_=pst[pb][:, 0:nt * 128].rearrange("p (t h d) -> p t h d", t=nt, h=2)),
                        reads=[R_ps[pb]], writes=[RQ])
                ws_done()
                for hh in range(2):
                    hd = 2 * hp + hh
                    biA = ws_next(("nbA", hd))
                    biB = ws_next(("nbB", hd))
                    tA = wb[:, biA, 0:1152].rearrange("p (t n) -> p t n", t=9)
                    tB = wb[:, biB, 0:1536].rearrange("p (t n) -> p t n", t=12)
                    for qb in range(16):
                        loc = nat_tiles(qb)
                        tiles = loc + [(16, None), (17, None)]
                        ntl = len(tiles)
                        pa, pbk = rot(), rot()
                        pi_ = pcnt % 2
                        pcnt += 1
                        nloc = len(loc)
                        tb0 = loc[0][1]
                        tsrc, tres, tofs = (tA, R_wb[biA], tb0) if tb0 < 9 else (tB, R_wb[biB], tb0 - 9)
                        for (bank, lo, hi) in ((pa, 0, 4), (pbk, 4, ntl)):
                            groups = []
                            for ti in range(lo, hi):
                                kt, tb = tiles[ti]
                                o_ = pst[bank][:, (ti - lo) * 128:(ti - lo + 1) * 128]
                                grp = [(o_, KZ[:, hh, kt * 128:(kt + 1) * 128], QT[:, qb * 128:(qb + 1) * 128])]
                                if lo > 0 and tb is not None:
                                    grp.append((o_, identb[:], tsrc[:, tofs + ti, :]))
                                groups.append(grp)
                            cx.mm_multi(groups, reads=[RQ, tres, R_identb], writes=[R_ps[bank]])
                            nb_ = (min(hi, nloc) - lo) if lo == 0 else 0
                            if nb_ > 0:
                                cx.op("dve", lambda e: e.tensor_tensor(
                                    out=pst[bank][:, 0:nb_ * 128].rearrange("p (t n) -> p t n", n=128),
                                    in0=pst[bank][:, 0:nb_ * 128].rearrange("p (t n) -> p t n", n=128),
                                    in1=tsrc[:, tofs + lo:tofs + lo + nb_, :], op=ALU.add),
                                    reads=[R_ps[bank], tres], writes=[R_ps[bank]])
                            cx.op("act", lambda e: e.activation(
                                out=PT[:, pi_, lo:hi, :], in_=pst[bank][:, 0:(hi - lo) * 128].rearrange("p (t n) -> p t n", n=128),
                                func=AF.Exp), reads=[R_ps[bank]], writes=[R_pt[pi_]])
                        def tail(pi_=pi_, tiles=tiles, ntl=ntl, qb=qb, hh=hh, ri_=pcnt % 4, VA=VA, RQ=RQ):
                            po = rot()
                            cx.mm_group([(pst[po][:, 0:65], PT[:, pi_, ti, :], VA[:, tiles[ti][0], hh, :]) for ti in range(ntl)],
                                        reads=[R_pt[pi_], RQ], writes=[R_ps[po]])
                            cx.op("dve", lambda e: e.reciprocal(out=rcp[:, ri_:ri_ + 1], in_=pst[po][:, 64:65]),
                                  reads=[R_ps[po]], writes=[R_rcp[ri_]])
                            cx.op("act", lambda e: e.activation(out=OTK[:, qb, hh, :], in_=pst[po][:, 0:64],
                                                                func=AF.Identity, scale=rcp[:, ri_:ri_ + 1]),
                                  reads=[R_ps[po], R_rcp[ri_]], writes=[R_ot])
                        if npend:
                            npend.pop()()
                        npend.append(tail)
                    ws_done()
                if npend:
                    npend.pop()()
                oi = hp % 2
                ptb = es.enter_context(nc.psum_tensor(f"ptb{hp}_{b}", [128, 1024], BF16)) if False else None
                for q4 in range(0, 16, 4):
                    pb = rot()
                    pv = pst[pb][:].bitcast(BF16)
                    cx.pe_raw([lambda e, qq=qq: e.transpose(pv[:, (qq - q4) * 128:(qq - q4 + 1) * 128],
                                                             OTK[:, qq].rearrange("p h d -> p (h d)"), identb[:])
                               for qq in range(q4, q4 + 4)], reads=[R_ot, R_identb], writes=[R_ps[pb]])
                    cx.op("act", lambda e: e.activation(out=OTH[:, oi, q4 * 128:(q4 + 4) * 128], in_=pv[:, 0:512],
                                                        func=AF.Identity), reads=[R_ps[pb]], writes=[R_oth[oi]])
                bi = ws_next(("nwo", hp))
                for m in range(KC):
                    for (t0, n) in TILES512:
                        if t0 >= SEQ:
                            continue
                        pb = rot()
                        cx.mm_group([(pst[pb][:, 0:n], wb[:, bi, m * 128:(m + 1) * 128], OTH[:, oi, t0:t0 + n])],
                                    reads=[R_wb[bi], R_oth[oi]], writes=[R_ps[pb]])
                        cx.op("dve", lambda e: e.scalar_tensor_tensor(
                            out=h[:, m, t0:t0 + n], in0=pst[pb][:, 0:n], scalar=mod[:, l, 16 + m, b:b + 1],
                            in1=h[:, m, t0:t0 + n], op0=ALU.mult, op1=ALU.add),
                            reads=[R_ps[pb], R_mod, R_h[m]], writes=[R_h[m]])
                ws_done()
            cx.barrier()

        def final_phase(b):
            goff = vec_cols["fg"][0]
            tiles = [t for t in TILES512 if t[0] < SEQ]
            cx.barrier()
            sqb = scr[:, 0:2].rearrange("p a n -> p (a n)").bitcast(BF16).rearrange("p (a n) -> p a n", a=4)
            R_sq = [cx.res(f"fsq{i}") for i in range(4)]
            NST = 8
            stg = big[:, 0:NST * 1024].bitcast(F32).rearrange("p (a n) -> p a n", a=NST)
            R_stg = [cx.res(f"fst{i}") for i in range(NST)]
            cnt = {"sq": 0, "st": 0}

            def stage_a(ti, t0, n):
                pb = 2 + ti % 2
                ri = ti % 2
                for k in range(KC):
                    si = cnt["sq"] % 4
                    cnt["sq"] += 1
                    cx.op("act", lambda e: e.activation(out=sqb[:, si, 0:n], in_=h[:, k, t0:t0 + n], func=AF.Square),
                          reads=[R_h[k]], writes=[R_sq[si]])
                    e = cx.E["pe"]
                    cx._deps(e, [R_sq[si], R_onesb], [R_ps[pb]] if k == 0 else [])
                    inst = e.h.matmul(pst[pb][:, 0:n], onesb[:], sqb[:, si, 0:n], start=(k == 0), stop=(k == KC - 1))
                    e.count += 1
                    inst.then_inc(e.sem, 1)
                    cx._register((e.sem, e.count), [R_sq[si], R_onesb], [R_ps[pb]])
                cx.op("act", lambda e: e.activation(out=rs[:, ri, 0:n], in_=pst[pb][:, 0:n], func=AF.Ln,
                                                    bias=V("epsv"), scale=1.0 / D),
                      reads=[R_ps[pb], R_vecs], writes=[R_rs[ri]])
                cx.op("act", lambda e: e.activation(out=rs[:, ri, 0:n], in_=rs[:, ri, 0:n], func=AF.Exp, scale=-0.5),
                      reads=[R_rs[ri]], writes=[R_rs[ri]])

            def stage_b(ti, t0, n):
                ri = ti % 2
                for k in range(KC):
                    si = cnt["st"] % NST
                    cnt["st"] += 1
                    if cfg.final_norm:
                        cx.op("dve", lambda e: e.scalar_tensor_tensor(
                            out=stg[:, si, 0:n], in0=h[:, k, t0:t0 + n], scalar=vecs[:, goff + k:goff + k + 1],
                            in1=rs[:, ri, 0:n], op0=ALU.mult, op1=ALU.mult),
                            reads=[R_h[k], R_vecs, R_rs[ri]], writes=[R_stg[si]])
                    else:
                        cx.op("dve", lambda e: e.tensor_copy(out=stg[:, si, 0:n], in_=h[:, k, t0:t0 + n]),
                              reads=[R_h[k]], writes=[R_stg[si]])
                    cx.dma("sp", yT[b, k * 128:(k + 1) * 128, t0:t0 + n], stg[:, si, 0:n], reads=[R_stg[si]])

            stage_a(0, *tiles[0])
            for ti in range(len(tiles)):
                if ti + 1 < len(tiles):
                    stage_a(ti + 1, *tiles[ti + 1])
                stage_b(ti, *tiles[ti])
            cx.wait_all("sp", R_stg)
            cx.barrier()

        for b in range(NB):
            cx.dma("sp", h[:, :, 0:SEQ], xT[b].rearrange("(k p) t -> p k t", p=128), writes=R_h)
            cx.dma("sp", h[:, :, SEQ:NTOK], cT[b].rearrange("(k p) t -> p k t", p=128), writes=R_h)
            for l in range(cfg.nl):
                kind = l % 3
                ctx_out = any((jj % 3) != 0 for jj in range(l + 1, DEPTH))
                ctx_in = ctx_out or kind != 0
                norm_phase(l, 0, b, ctx_in)
                if kind == 0:
                    pool_phase(l, b, ctx_out)
                elif kind == 1:
                    s5_phase(l, b)
                else:
                    nat_phase(l, b)
                norm_phase(l, 1, b, ctx_out)
                ffn_phase(l, b, ctx_out)
            final_phase(b)
        print("instructions:", cx.n_inst, "sems left:", len(cx.free_sems), "sbuf free:", nc.sbuf_bytes_remaining)
    return nc


FFN_GROUPS = [list(range(0, 6)), list(range(6, 12)), list(range(12, 17)), list(range(17, 22))]


def make_bias_tables(rpb):
    cases = [(2, 0 + i) for i in range(5)]
    cases += [(0, i) for i in range(4)] + [(1, i) for i in range(4)]
    cases += [(14, 12 + i) for i in range(4)] + [(15, 12 + i) for i in range(4)]
    out = np.empty((16, 128, 21, 128), np.float32)
    kk = np.arange(128)
    for t, (qb, kt) in enumerate(cases):
        krow = 2 * kt + kk // 64
        kcol = kk % 64
        qrow = 2 * qb + kk // 64
        qcol = kk % 64
        rs = np.clip(qrow - 4, 0, 24)
        cs = np.clip(qcol - 8, 0, 48)
        rvalid = (krow[:, None] >= rs[None, :]) & (krow[:, None] < rs[None, :] + 8)
        cvalid = (kcol[:, None] >= cs[None, :]) & (kcol[:, None] < cs[None, :] + 16)
        roff = np.clip(krow[:, None] - qrow[None, :] + 7, 0, 14)
        coff = np.clip(kcol[:, None] - qcol[None, :], -15, 15) + 15
        g = rpb[:, roff, coff]
        out[:, :, t, :] = np.where((rvalid & cvalid)[None], g, np.float32(-1e30))
    return out


def prep_shared(inp):
    vp = make_vecs(inp)
    vp.add("epsv", np.full((128, 1), EPS, np.float32))
    pm = np.zeros((128, 2), np.float32)
    pm[0:64, 0] = 1.0
    pm[64:128, 1] = 1.0
    vp.add("pm0", pm[:, 0:1])
    vp.add("pm1", pm[:, 1:2])
    shared = {
        "vecs": vp.pack(),
        "ada_w": np.ascontiguousarray(inp["ada_w"], np.float32),
        "wup": None,
        "wdn": np.ascontiguousarray(inp["ffn_w_down"], np.float32),
        "pool_w": np.ascontiguousarray(inp["pool_w"], np.float32),
        "s5w1": np.ascontiguousarray(inp["s5_w1"][0], np.float32),
        "s5w2": np.ascontiguousarray(inp["s5_w2"][0], np.float32),
    }
    shared["nqkv"] = np.ascontiguousarray(inp["nat_w_qkv"][0], np.float32)
    shared["nwo"] = np.ascontiguousarray(inp["nat_w_o"][0], np.float32)
    shared["btab"] = make_bias_tables(np.asarray(inp["nat_rpb"], np.float32)[0])
    a_re = np.asarray(inp["s5_a_re"], np.float32)[0]
    a_im = np.asarray(inp["s5_a_im"], np.float32)[0]
    ldt = np.asarray(inp["s5_log_dt"], np.float32)[0]
    s5a = np.zeros((128, 3, 64), np.float32)
    s5a[:, 0] = a_re.transpose(0, 2, 1).reshape(128, 64)
    s5a[:, 1] = a_im.transpose(0, 2, 1).reshape(128, 64)
    s5a[:, 2] = np.repeat(ldt[:, None, :], 64, axis=1).reshape(128, 64)
    shared["s5a"] = s5a.reshape(128, 192)
    s5bc = np.zeros((128, 4, 64, 16), np.float32)
    s5bc[:, 0] = np.asarray(inp["s5_b_re"], np.float32)[0].transpose(0, 2, 1, 3).reshape(128, 64, 16)
    s5bc[:, 1] = np.asarray(inp["s5_b_im"], np.float32)[0].transpose(0, 2, 1, 3).reshape(128, 64, 16)
    s5bc[:, 2] = np.asarray(inp["s5_c_re"], np.float32)[0].transpose(0, 3, 1, 2).reshape(128, 64, 16)
    s5bc[:, 3] = np.asarray(inp["s5_c_im"], np.float32)[0].transpose(0, 3, 1, 2).reshape(128, 64, 16)
    shared["s5bc"] = s5bc
    cst = np.zeros((128, 386), np.float32)
    cst[0:64, 384] = 1.0
    cst[64:128, 385] = 1.0
    cst[:, 0:128] = np.eye(128, dtype=np.float32)
    sp = np.arange(128) // 16
    cst[:, 128:256] = (sp[None, :] >= sp[:, None])
    cst[:, 256:384] = (sp[None, :] <= sp[:, None])
    shared["cst"] = cst
    selm = np.zeros((128, 64, 128), np.float32)
    for q in range(4):
        for par in range(2):
            for c in range(16):
                row = 32 * q + par * 16 + c
                for s_ in range(8):
                    selm[row, par * 8 + s_, s_ * 16 + c] = 1.0
                for gl in range(8):
                    selm[row, 16 + gl * 2 + par, gl * 16 + c] = 1.0
    selm[96:128, 32:64, :] = selm[96:128, 0:32, :]
    shared["sel"] = selm.reshape(128, 64 * 128)
    w = np.asarray(inp["ffn_w_up"], np.float32).reshape(DEPTH, KC, 128, 2, NFC, 128)
    shared["wup"] = np.ascontiguousarray(w.transpose(0, 4, 2, 1, 3, 5)).reshape(DEPTH, NFC, 128, KC * 256)
    return shared, vp.cols, vp.n


def prep_core(inp, bs):
    x = np.asarray(inp["x"], np.float32)[bs]
    ctx = np.asarray(inp["ctx"], np.float32)[bs]
    c = np.asarray(inp["c"], np.float32)[bs]
    nb = len(bs)
    cond = np.zeros((5, D), np.float32)
    cond[:nb] = c
    cond[4] = np.asarray(inp["c_ctx"], np.float32)
    condT = np.ascontiguousarray(cond.reshape(5, KC, 128).transpose(2, 1, 0)).reshape(128, KC * 5)
    return {
        "xT": np.ascontiguousarray(x.transpose(0, 2, 1)),
        "ctxT": np.ascontiguousarray(ctx.transpose(0, 2, 1)),
        "cond": condT,
    }


def run(inp, cfg, core_batches, trace=False):
    shared, cols, nvec = prep_shared(inp)
    nc = build_program(cfg, cols, nvec)
    in_maps = []
    for bs in core_batches:
        m = dict(shared)
        m.update(prep_core(inp, bs))
        in_maps.append(m)
    res = run_bass_kernel_spmd(nc, in_maps, core_ids=list(range(len(core_batches))), trace=trace)
    if trace:
        print("EXEC_TIME_NS", res.exec_time_ns)
    outs = [np.asarray(r["yT"]).transpose(0, 2, 1) for r in res.results]
    return np.ascontiguousarray(np.concatenate(outs, axis=0))


def kernel(**inputs):
    cfg = Cfg(nb=4, nl=4)
    core_batches = [list(range(4 * i, 4 * i + 4)) for i in range(8)]
    return run(inputs, cfg, core_batches).astype(np.float32)
```

```python
import numpy as np
import concourse.bass as bass
import concourse.mybir as mybir
from concourse.bass_utils import run_bass_kernel_spmd

F32 = mybir.dt.float32
BF16 = mybir.dt.bfloat16
AF = mybir.ActivationFunctionType
ALU = mybir.AluOpType

D = 1024
KC = 8
SEQ = 2048
CTX = 256
NTOK = SEQ + CTX
DEPTH = 4
DFF = 2816
NFC = 22
POOL_WINDOWS = (2, 4, 8, 16)
EPS = 1e-6
LAT0 = 8
CTX0 = 8 + SEQ + 16
UCOLS = CTX0 + CTX + 8


class Res:
    __slots__ = ("name", "w", "r", "dsem")

    def __init__(self, name):
        self.name = name
        self.w = None
        self.r = []
        self.dsem = None


class EngState:
    def __init__(self, name, h, sem):
        self.name = name
        self.h = h
        self.sem = sem
        self.count = 0
        self.waited = {}


class Ctx:
    def __init__(self, nc, sems):
        self.nc = nc
        self.free_sems = list(sems)
        self.E = {}
        for name, h in (("pe", nc.tensor), ("act", nc.scalar), ("dve", nc.vector),
                        ("pool", nc.gpsimd), ("sp", nc.sync)):
            self.E[name] = EngState(name, h, self.free_sems.pop())
        self.dtot = {}
        self.n_inst = 0

    def res(self, name):
        return Res(name)

    def _wait(self, e, tok):
        sem, val = tok
        if sem in self.dtot:
            val = self.dtot[sem]
        key = id(sem)
        if e.waited.get(key, 0) < val:
            e.h.wait_ge(sem, val)
            e.waited[key] = val

    def _deps(self, e, reads, writes):
        for r in reads:
            if r.w is not None:
                self._wait(e, r.w)
        for w in writes:
            if w.w is not None:
                self._wait(e, w.w)
            for t in w.r:
                self._wait(e, t)

    def _register(self, tok, reads, writes):
        for r in reads:
            r.r.append(tok)
        for w in writes:
            w.w = tok
            w.r = []

    def op(self, eng, fn, reads=(), writes=()):
        e = self.E[eng]
        self._deps(e, reads, writes)
        inst = fn(e.h)
        e.count += 1
        inst.then_inc(e.sem, 1)
        self._register((e.sem, e.count), reads, writes)
        self.n_inst += 1
        return inst

    def mm_group(self, mms, reads, writes, transpose=False):
        e = self.E["pe"]
        self._deps(e, reads, writes)
        n = len(mms)
        inst = None
        for i, (o, l, r) in enumerate(mms):
            inst = e.h.matmul(o, l, r, start=(i == 0), stop=(i == n - 1))
        e.count += 1
        inst.then_inc(e.sem, 1)
        self._register((e.sem, e.count), reads, writes)
        self.n_inst += n

    def mm_multi(self, groups, reads, writes):
        e = self.E["pe"]
        self._deps(e, reads, writes)
        inst = None
        cnt = 0
        for mms in groups:
            n = len(mms)
            for i, (o, l, r) in enumerate(mms):
                inst = e.h.matmul(o, l, r, start=(i == 0), stop=(i == n - 1))
                cnt += 1
        e.count += 1
        inst.then_inc(e.sem, 1)
        self._register((e.sem, e.count), reads, writes)
        self.n_inst += cnt

    def barrier(self):
        for e in self.E.values():
            for o in self.E.values():
                if o is not e and o.count > 0:
                    self._wait(e, (o.sem, o.count))
            for sem, tot in self.dtot.items():
                if tot > 0:
                    self._wait(e, (sem, tot))

    def pe_raw(self, fns, reads, writes):
        e = self.E["pe"]
        self._deps(e, reads, writes)
        inst = None
        for fn in fns:
            inst = fn(e.h)
        e.count += 1
        inst.then_inc(e.sem, 1)
        self._register((e.sem, e.count), reads, writes)
        self.n_inst += len(fns)

    def dma(self, q, out, in_, reads=(), writes=(), **kw):
        e = self.E[q]
        self._deps(e, reads, writes)
        owner = writes[0] if writes else reads[0]
        if owner.dsem is None:
            owner.dsem = self.free_sems.pop()
            self.dtot[owner.dsem] = 0
        sem = owner.dsem
        self.dtot[sem] += 16
        e.h.dma_start(out=out, in_=in_, **kw).then_inc(sem, 16)
        self._register((sem, self.dtot[sem]), reads, writes)
        self.n_inst += 1

    def wait_all(self, eng, ress):
        e = self.E[eng]
        for r in ress:
            if r.w is not None:
                self._wait(e, r.w)
            for t in r.r:
                self._wait(e, t)


def _fm(v):
    v = np.asarray(v, np.float32)
    lead = v.shape[:-1]
    n = v.shape[-1] // 128
    v = v.reshape(lead + (n, 128))
    v = np.moveaxis(v, -1, 0)
    return np.ascontiguousarray(v).reshape(128, -1)


class VecPack:
    def __init__(self):
        self.cols = {}
        self.parts = []
        self.n = 0

    def add(self, name, arr):
        arr = np.ascontiguousarray(arr, np.float32).reshape(128, -1)
        self.cols[name] = (self.n, arr.shape[1])
        self.parts.append(arr)
        self.n += arr.shape[1]

    def pack(self):
        return np.ascontiguousarray(np.concatenate(self.parts, axis=1))


VEC_LAYOUT = None


def make_vecs(inp):
    vp = VecPack()
    vp.add("n1g", _fm(inp["norm1_g"]))
    vp.add("n2g", _fm(inp["norm2_g"]))
    vp.add("fg", _fm(inp["final_g"]))
    vp.add("adab", _fm(inp["ada_b"]))
    vp.add("cw", _fm(inp["ffn_conv_w"]))
    vp.add("cb", _fm(inp["ffn_conv_b"]))
    vp.add("pb", _fm(inp["pool_b"]))
    vp.add("ps", _fm(inp["pool_scale"]))
    vp.add("s5d", _fm(inp["s5_d"]))
    vp.add("s5b1", _fm(inp["s5_b1"]))
    vp.add("s5b2", _fm(inp["s5_b2"]))
    tbl = np.zeros((4, 2, 8), np.float32)
    for wi, w in enumerate(POOL_WINDOWS):
        for t in range(w // 2):
            tbl[wi, 0, t] = 1.0 / (w // 2 + t)
        for i in range(w // 2 - 1):
            tbl[wi, 1, i] = 1.0 / (w - 1 - i)
    vp.add("ptbl", np.broadcast_to(tbl.reshape(1, -1), (128, 64)))
    return vp


class Cfg:
    def __init__(self, nb=4, nl=4, final_norm=True):
        self.nb = nb
        self.nl = nl
        self.final_norm = final_norm


def build_program(cfg, vec_cols, nvec):
    nc = bass.Bass("TRN2", target_bir_lowering=False, dynamic_dma_scratch_size=4096)
    NB = cfg.nb

    def dram_in(name, shape, dt=F32):
        return nc.dram_tensor(name, list(shape), dt, kind="ExternalInput").ap()

    xT = dram_in("xT", [NB, D, SEQ])
    cT = dram_in("ctxT", [NB, D, CTX])
    cond = dram_in("cond", [128, KC * 5])
    vecs_d = dram_in("vecs", [128, nvec])
    ada_w = dram_in("ada_w", [DEPTH, D, 6 * D])
    wup = dram_in("wup", [DEPTH, NFC, 128, KC * 256])
    wdn = dram_in("wdn", [DEPTH, DFF, D])
    pool_w = dram_in("pool_w", [2, 4, 256, 256])
    s5a_d = dram_in("s5a", [128, 3 * 64])
    s5bc_d = dram_in("s5bc", [128, 4, 64, 16])
    cst_d = dram_in("cst", [128, 386])
    sel_d = dram_in("sel", [128, 64 * 128])
    s5w1_d = dram_in("s5w1", [D, D])
    s5w2_d = dram_in("s5w2", [D, D])
    nqkv_d = dram_in("nqkv", [D, 3 * D])
    nwo_d = dram_in("nwo", [D, D])
    btab_d = dram_in("btab", [16, 128, 21, 128])
    a16_d = nc.dram_tensor("a16_scr", [128, 256], F32, kind="Internal").ap()
    s5w_d = nc.dram_tensor("s5w_scr", [64, 128, 896], BF16, kind="Internal").ap()
    yT = nc.dram_tensor("yT", [NB, D, SEQ], F32, kind="ExternalOutput").ap()

    import contextlib
    es = contextlib.ExitStack()
    with es:
        sems = [es.enter_context(nc.semaphore(f"s{i}")) for i in range(100)]
        cx = Ctx(nc, sems)

        def sb(name, shape, dt):
            return es.enter_context(nc.sbuf_tensor(name, list(shape), dt))

        h = sb("h", [128, KC, NTOK], F32)
        R_h = [cx.res(f"h{k}") for k in range(KC)]
        u = sb("u", [128, KC, UCOLS], BF16)
        R_u = [cx.res(f"u{k}") for k in range(KC)]
        vecs = sb("vecs_sb", [128, nvec], F32)
        R_vecs = cx.res("vecs")
        mod = sb("mod", [128, DEPTH, 48, 5], F32)
        R_mod = cx.res("mod")
        gm = sb("gm", [128, DEPTH, 2, KC, 5], F32)
        R_gm = cx.res("gm")
        ones = sb("ones", [128, 128], F32)
        R_ones = cx.res("ones")
        onesb = sb("onesb", [128, 128], BF16)
        R_onesb = cx.res("onesb")
        rs = sb("rs", [128, 2, 512], F32)
        R_rs = [cx.res("rs0"), cx.res("rs1")]
        scr = sb("scr", [128, 4, 512], F32)
        R_scr = [cx.res(f"scr{i}") for i in range(4)]
        sv = sb("sv", [128, 64], F32)
        R_sv = cx.res("sv")
        NWB = 4
        wb = sb("wb", [128, NWB, 2048], BF16)
        R_wb = [cx.res(f"wb{i}") for i in range(NWB)]
        BIGE = 27776
        big = sb("big", [128, BIGE], BF16)
        R_big = [cx.res("big0"), cx.res("big1")]

        sel = sb("sel_sb", [128, 64, 128], BF16)
        R_sel = cx.res("sel")
        a8m = sb("a8m", [128, 2, 2, 64], F32)
        R_a8 = cx.res("a8")
        hst = sb("hst", [128, 3, 2, 32], F32)
        identb = sb("identb", [128, 128], BF16)
        R_identb = cx.res("identb")
        rcp = sb("rcp", [128, 4], F32)
        R_rcp = [cx.res(f"rcp{i}") for i in range(4)]
        R_hst = [cx.res("hs0"), cx.res("hs1"), cx.res("hs2")]
        R_s5w = cx.res("s5w")
        R_a16d = cx.res("a16d")
        pst = [es.enter_context(nc.psum_tensor(f"ps{i}", [128, 512], F32)) for i in range(8)]
        R_ps = [cx.res(f"ps{i}") for i in range(8)]

        def V(name, idx=0):
            off, n = vec_cols[name]
            return vecs[:, off + idx: off + idx + 1]

        cx.dma("sp", vecs[:], vecs_d[:, :], writes=[R_vecs])
        condt = sb("condt", [128, KC, 5], F32)
        R_cond = cx.res("cond")
        cx.dma("sp", condt[:].rearrange("p k j -> p (k j)"), cond[:, :], writes=[R_cond])
        cx.op("dve", lambda e: e.memset(ones[:], 1.0), writes=[R_ones])
        cx.op("dve", lambda e: e.memset(onesb[:], 1.0), writes=[R_onesb])
        cx.dma("pool", identb[:], cst_d[:, 0:128], writes=[R_identb])
        for k in range(KC):
            cx.op("pool", lambda e, k=k: e.memset(u[:, k, :], 0.0), writes=[R_u[k]])
        sct = sb("sct", [128, KC, 5], F32)
        R_sct = cx.res("sct")
        cx.op("act", lambda e: e.activation(out=sct[:], in_=condt[:], func=AF.Silu),
              reads=[R_cond], writes=[R_sct])

        APW = 384
        wa = big[:, 0:2 * KC * APW * 2].bitcast(F32).rearrange("p (a kc n) -> p a kc n", a=2, kc=KC)
        R_wa = R_big
        ada_v = ada_w.rearrange("l (kc p) n -> l p kc n", p=128)
        pieces = [(l, pc) for l in range(cfg.nl) for pc in range(6 * D // APW)]
        for i, (l, pc) in enumerate(pieces[:2]):
            cx.dma("act", wa[:, i % 2], ada_v[l, :, :, pc * APW:(pc + 1) * APW], writes=[R_wa[i % 2]])
        ada_todo = []

        def ada_piece(i, l, pc):
            bi = i % 2
            for m in range(APW // 128):
                idx = pc * (APW // 128) + m
                pb = idx % 2
                mms = [(pst[pb][:, 0:5], wa[:, bi, kc, m * 128:(m + 1) * 128], sct[:, kc, :])
                       for kc in range(KC)]
                cx.mm_group(mms, reads=[R_wa[bi], R_sct], writes=[R_ps[pb]])
                cx.op("act", lambda e, l=l, idx=idx, pb=pb: e.activation(
                    out=mod[:, l, idx, :], in_=pst[pb][:, 0:5], func=AF.Identity,
                    bias=V("adab", l * 48 + idx), scale=1.0),
                    reads=[R_ps[pb], R_vecs], writes=[R_mod])
            if i + 2 < len(pieces):
                l2, pc2 = pieces[i + 2]
                cx.dma("act", wa[:, bi], ada_v[l2, :, :, pc2 * APW:(pc2 + 1) * APW], writes=[R_wa[bi]])

        for i, (l, pc) in enumerate(pieces):
            ada_todo.append((i, l, pc))

        def ada_step(frac):
            ntodo = int(round(len(pieces) * frac))
            while ada_todo and ntodo > 0:
                ada_piece(*ada_todo.pop(0))
                ntodo -= 1

        def ada_finish():
            while ada_todo:
                ada_piece(*ada_todo.pop(0))
            ada_gm()

        def ada_gm():
          for l in range(cfg.nl):
            for which, (gname, scoff) in enumerate((("n1g", 8), ("n2g", 32))):
                goff = vec_cols[gname][0] + l * KC
                gb = vecs[:, goff:goff + KC].unsqueeze(2).to_broadcast([128, KC, 5])
                cx.op("dve", lambda e, l=l, which=which, scoff=scoff, gb=gb: e.scalar_tensor_tensor(
                    out=gm[:, l, which], in0=mod[:, l, scoff:scoff + KC, :], scalar=1.0, in1=gb,
                    op0=ALU.add, op1=ALU.mult),
                    reads=[R_mod, R_vecs], writes=[R_gm])

        def s5_gen():
            R_g = cx.res("s5gen")
            hs = h[:].rearrange("p k t -> p (k t)")
            st = {"off": 0}

            def carve(n):
                ap = hs[:, st["off"]:st["off"] + n]
                st["off"] += n
                return ap

            def tt(out, a, b_, op, eng="dve"):
                cx.op(eng, lambda e: e.tensor_tensor(out=out, in0=a, in1=b_, op=op), reads=[R_g, R_vecs], writes=[R_g])

            def ts(out, a, s1, op0, s2=None, op1=None):
                if op1 is None:
                    cx.op("dve", lambda e: e.tensor_scalar(out=out, in0=a, scalar1=s1, scalar2=None, op0=op0),
                          reads=[R_g], writes=[R_g])
                else:
                    cx.op("dve", lambda e: e.tensor_scalar(out=out, in0=a, scalar1=s1, scalar2=s2, op0=op0, op1=op1),
                          reads=[R_g], writes=[R_g])

            def act(out, a, func, scale=1.0, bias=None):
                if bias is None:
                    cx.op("act", lambda e: e.activation(out=out, in_=a, func=func, scale=scale), reads=[R_g], writes=[R_g])
                else:
                    cx.op("act", lambda e: e.activation(out=out, in_=a, func=func, scale=scale, bias=bias),
                          reads=[R_g, R_vecs], writes=[R_g])

            a3 = carve(192).rearrange("p (a g) -> p a g", a=3)
            cstt = carve(386)
            cx.dma("sp", a3.rearrange("p a g -> p (a g)"), s5a_d[:, :], writes=[R_g])
            cx.dma("sp", cstt, cst_d[:, :], writes=[R_g])
            cx.dma("pool", sel[:].rearrange("p a n -> p (a n)"), sel_d[:, :], writes=[R_sel], max_dma_last_dim=4096)
            ident = cstt[:, 0:128]
            mfr = cstt[:, 128:384]
            import os as _os3
            GD = int(_os3.environ.get("GENDBG", "9"))
            if GD < 1:
                return
            G = 64
            lr = carve(G); dt = carve(G); er = carve(G); th = carve(G)
            t1 = carve(G); t2 = carve(G); t3 = carve(G); mag = carve(G)
            pr = carve(9 * G).rearrange("p (k g) -> p k g", k=9)
            pi = carve(9 * G).rearrange("p (k g) -> p k g", k=9)
            ir = carve(G); ii = carve(G); qr = carve(G); qi = carve(G)
            ts(lr, a3[:, 0, :], -1e-4, ALU.min)
            act(dt, a3[:, 2, :], AF.Exp)
            tt(er, lr, dt, ALU.mult)
            tt(th, a3[:, 1, :], dt, ALU.mult)
            ts(th, th, 1.0 / (2.0 * np.pi), ALU.mult)
            MAGIC = 12582912.0

            def power(k, o_r, o_i):
                act(mag, er, AF.Exp, scale=float(k))
                for (dst, shift) in ((t3, 0.0), (t2, 0.25)):
                    ts(t1, th, float(k), ALU.mult, shift, ALU.add)
                    ts(dst, t1, MAGIC, ALU.add, MAGIC, ALU.subtract)
                    tt(t1, t1, dst, ALU.subtract)
                    act(dst, t1, AF.Sin, scale=2.0 * np.pi)
                tt(o_r, mag, t2, ALU.mult)
                tt(o_i, mag, t3, ALU.mult)

            if GD < 2:
                return
            cx.op("dve", lambda e: e.memset(pr[:, 0, :], 1.0), reads=[R_g], writes=[R_g])
            cx.op("dve", lambda e: e.memset(pi[:, 0, :], 0.0), reads=[R_g], writes=[R_g])
            for k in range(1, 9):
                power(k, pr[:, k, :], pi[:, k, :])
            power(-8, ir, ii)
            cx.op("dve", lambda e: e.tensor_copy(out=a8m[:, 0, 0, :], in_=pr[:, 8, :]), reads=[R_g], writes=[R_a8])
            cx.op("dve", lambda e: e.tensor_copy(out=a8m[:, 0, 1, :], in_=pr[:, 8, :]), reads=[R_g], writes=[R_a8])
            cx.op("dve", lambda e: e.tensor_copy(out=a8m[:, 1, 0, :], in_=pi[:, 8, :]), reads=[R_g], writes=[R_a8])
            cx.op("dve", lambda e: e.tensor_scalar(out=a8m[:, 1, 1, :], in0=pi[:, 8, :], scalar1=-1.0, scalar2=None,
                                                   op0=ALU.mult), reads=[R_g], writes=[R_a8])
            if GD < 3:
                return
            a16t = carve(4 * G).rearrange("p (c r g) -> p c r g", c=2, r=2)
            power(16, a16t[:, 0, 0, :], a16t[:, 1, 0, :])
            cx.op("dve", lambda e: e.tensor_copy(out=a16t[:, 0, 1, :], in_=a16t[:, 0, 0, :]), reads=[R_g], writes=[R_g])
            cx.op("dve", lambda e: e.tensor_scalar(out=a16t[:, 1, 1, :], in0=a16t[:, 1, 0, :], scalar1=-1.0, scalar2=None,
                                                   op0=ALU.mult), reads=[R_g], writes=[R_g])
            cx.dma("sp", a16_d[:, :], a16t.rearrange("p c r g -> p (c r g)"), reads=[R_g], writes=[R_a16d])
            am1 = carve(G); den = carve(G)
            li = a3[:, 1, :]
            ts(am1, pr[:, 1, :], -1.0, ALU.add)
            tt(t1, lr, lr, ALU.mult)
            tt(t2, li, li, ALU.mult)
            tt(den, t1, t2, ALU.add)
            cx.op("dve", lambda e: e.reciprocal(out=den, in_=den), reads=[R_g], writes=[R_g])
            tt(t1, am1, lr, ALU.mult)
            tt(t2, pi[:, 1, :], li, ALU.mult)
            tt(t1, t1, t2, ALU.add)
            tt(qr, t1, den, ALU.mult)
            tt(t1, pi[:, 1, :], lr, ALU.mult)
            tt(t2, am1, li, ALU.mult)
            tt(t1, t1, t2, ALU.subtract)
            tt(qi, t1, den, ALU.mult)

            GB = 16
            bcb = carve(4 * GB * 16).rearrange("p (a g c) -> p a g c", a=4, g=GB)
            Bbr = carve(GB * 16).rearrange("p (g c) -> p g c", g=GB)
            Bbi = carve(GB * 16).rearrange("p (g c) -> p g c", g=GB)
            u1 = carve(GB * 16).rearrange("p (g c) -> p g c", g=GB)
            u2 = carve(GB * 16).rearrange("p (g c) -> p g c", g=GB)
            PB = carve(2 * GB * 128).rearrange("p (r g s c) -> p r g s c", r=2, g=GB, s=8)
            Et = carve(2 * GB * 128).rearrange("p (r g n) -> p r g n", r=2, g=GB)
            Fm = carve(2 * GB * 128).rearrange("p (r g s c) -> p r g s c", r=2, g=GB, s=8)
            t12 = carve(256)
            Fz = carve(2 * 2 * 128).rearrange("p (h r n) -> p h r n", h=2, r=2)
            assert st["off"] <= KC * NTOK, st["off"]
            stage = big[:, 12288:12288 + GB * 896].rearrange("p (g n) -> p g n", g=GB)
            R_stage = cx.res("stage")

            def bc_g(ap2d):
                return ap2d.unsqueeze(2).to_broadcast([ap2d.shape[0], GB, 16])

            for blk in range(64 // GB if GD >= 4 else 0):
                gs = slice(blk * GB, (blk + 1) * GB)
                ada_step(0.2)
                cx.dma("sp", bcb, s5bc_d[:, :, gs, :], writes=[R_g])
                Bre, Bim, Cre, Cim = bcb[:, 0], bcb[:, 1], bcb[:, 2], bcb[:, 3]
                tt(u1, Bre, bc_g(qr[:, gs]), ALU.mult)
                tt(u2, Bim, bc_g(qi[:, gs]), ALU.mult)
                tt(Bbr, u1, u2, ALU.subtract)
                tt(u1, Bim, bc_g(qr[:, gs]), ALU.mult)
                tt(u2, Bre, bc_g(qi[:, gs]), ALU.mult)
                tt(Bbi, u1, u2, ALU.add)
                for s_ in range(8):
                    for half in range(2):
                        rows = slice(64 * half, 64 * half + 64)
                        k = (7 - s_) if half == 0 else s_
                        pkr = bc_g(pr[rows, k, gs]); pki = bc_g(pi[rows, k, gs])
                        tt(u1[rows], Bbr[rows], pkr, ALU.mult)
                        tt(u2[rows], Bbi[rows], pki, ALU.mult)
                        tt(PB[rows, 0, :, s_, :], u1[rows], u2[rows], ALU.subtract)
                        tt(u1[rows], Bbi[rows], pkr, ALU.mult)
                        tt(u2[rows], Bbr[rows], pki, ALU.mult)
                        tt(PB[rows, 1, :, s_, :], u1[rows], u2[rows], ALU.add)
                        k = (s_ + 1) if half == 0 else (8 - s_)
                        pkr = bc_g(pr[rows, k, gs]); pki = bc_g(pi[rows, k, gs])
                        tt(u1[rows], Cre[rows], pkr, ALU.mult)
                        tt(u2[rows], Cim[rows], pki, ALU.mult)
                        tt(Fm[rows, 0, :, s_, :], u1[rows], u2[rows], ALU.subtract)
                        tt(u1[rows], Cim[rows], pkr, ALU.mult)
                        tt(u2[rows], Cre[rows], pki, ALU.mult)
                        cx.op("dve", lambda e: e.scalar_tensor_tensor(
                            out=Fm[rows, 1, :, s_, :], in0=u1[rows], scalar=-1.0, in1=u2[rows],
                            op0=ALU.mult, op1=ALU.subtract), reads=[R_g], writes=[R_g])
                PBf = PB.rearrange("p r g s c -> p r g (s c)")
                Ff = Fm.rearrange("p r g s c -> p r g (s c)")

                def bc_n(ap2d):
                    return ap2d.unsqueeze(2).to_broadcast([128, GB, 128])
                tt(Et[:, 0], PBf[:, 0], bc_n(ir[:, gs]), ALU.mult)
                tt(Et[:, 1], PBf[:, 1], bc_n(ii[:, gs]), ALU.mult)
                tt(Et[:, 0], Et[:, 0], Et[:, 1], ALU.subtract)
                tt(Et[:, 1], PBf[:, 1], bc_n(ir[:, gs]), ALU.mult)
                for gg in range(GB):
                    tt(t12[:, 0:128], PBf[:, 0, gg, :], ii[:, blk * GB + gg: blk * GB + gg + 1].to_broadcast([128, 128]), ALU.mult)
                    tt(Et[:, 1, gg, :], Et[:, 1, gg, :], t12[:, 0:128], ALU.add)
                if GD < 5:
                    continue
                for dr in range(2):
                    for ri in range(2):
                        c0_ = 384 + dr * 256 + ri * 128
                        cx.op("dve", lambda e: e.tensor_scalar(
                            out=stage[:, :, c0_:c0_ + 128], in0=Ff[:, ri], scalar1=cstt[:, 384 + dr:385 + dr],
                            scalar2=None, op0=ALU.mult), reads=[R_g], writes=[R_stage])
                for gg in range(GB if GD >= 6 else 0):
                    pb = gg % 2
                    G6 = _os3.environ.get("G6", "tm")
                    if "t" in G6:
                        cx.pe_raw([lambda e, ri=ri: e.transpose(pst[pb][:, ri * 128:(ri + 1) * 128], PBf[:, ri, gg, :], ident)
                                   for ri in range(2)], reads=[R_g], writes=[R_ps[pb]])
                        cx.op("act", lambda e: e.activation(out=stage[:, gg, 0:256], in_=pst[pb][:, 0:256], func=AF.Identity),
                              reads=[R_ps[pb]], writes=[R_stage])
                    if "m" not in G6:
                        continue
                    pb2 = 2 + gg % 2
                    groups = []
                    for half in range(2):
                        cx.op("dve", lambda e: e.tensor_scalar(
                            out=Fz[:, half], in0=Ff[:, :, gg, :], scalar1=cstt[:, 384 + half:385 + half], scalar2=None,
                            op0=ALU.mult), reads=[R_g], writes=[R_g])
                        groups.append([(pst[pb2][:, half * 128:(half + 1) * 128], Et[:, ri, gg, :], Fz[:, half, ri, :])
                                       for ri in range(2)])
                    cx.mm_multi(groups, reads=[R_g], writes=[R_ps[pb2]])
                    cx.op("dve", lambda e: e.tensor_tensor(out=t12, in0=pst[pb2][:, 0:256], in1=mfr, op=ALU.mult),
                          reads=[R_ps[pb2], R_g], writes=[R_g])
                    cx.op("dve", lambda e: e.tensor_tensor(out=stage[:, gg, 256:384], in0=t12[:, 0:128], in1=t12[:, 128:256],
                                                           op=ALU.add), reads=[R_g], writes=[R_stage])
                if GD >= 7:
                    cx.dma("sp", s5w_d.rearrange("g p n -> p g n")[:, gs, :], stage, reads=[R_stage], writes=[R_s5w])

        import os as _os
        S5DBG = int(_os.environ.get("S5DBG", "9"))
        ada_step(0.15)
        if cfg.nl > 1 and S5DBG >= 0:
            s5_gen()
        ada_finish()
        cx.barrier()

        plan = []
        for b in range(NB):
            for l in range(cfg.nl):
                kind = l % 3
                if kind == 0:
                    plan.append(("pool", l // 3))
                elif kind == 1:
                    import os as _os2
                    _d = int(_os2.environ.get("S5DBG", "9"))
                    for ps_ in range(2 if _d >= 1 else 0):
                        for i in range(4):
                            plan.append(("s5wb", ps_, i))
                        for i in range(11 if _d >= 4 else 0):
                            plan.append(("s5we", ps_, i))
                    for m in range(KC if _d >= 6 else 0):
                        plan.append(("glu", m))
                elif kind == 2:
                    for hp in range(8):
                        plan.append(("nqk", hp))
                        plan.append(("nv", hp))
                        for hh in range(2):
                            plan.append(("nbA", 2 * hp + hh))
                            plan.append(("nbB", 2 * hp + hh))
                        plan.append(("nwo", hp))
                for g in range(4):
                    prs = FFN_GROUPS[g]
                    for j in prs:
                        plan.append(("wup", l, j))
                    for jj in range(0, len(prs), 2):
                        plan.append(("wdn", l, prs[jj:jj + 2]))
        ws_state = {"issued": 0, "next": 0}

        def ws_issue():
            i = ws_state["issued"]
            if i >= len(plan):
                return
            item = plan[i]
            bi = i % NWB
            kind = item[0]
            if kind == "wup":
                _, l, j = item
                cx.dma("pool", wb[:, bi, :], wup[l, j], writes=[R_wb[bi]], max_dma_last_dim=4096)
            elif kind == "wdn":
                _, l, js = item
                src = wdn[l].rearrange("(fc p) d -> p fc d", p=128)[:, js[0]:js[0] + len(js), :]
                dst = wb[:, bi, 0:len(js) * 1024].rearrange("p (a d) -> p a d", d=1024)
                cx.dma("pool", dst, src, writes=[R_wb[bi]], max_dma_last_dim=4096)
            elif kind == "s5wb":
                _, ps_, i = item
                g0 = ps_ * 32 + i * 8
                src = s5w_d[g0:g0 + 8, :, 0:256].rearrange("g p n -> p g n")
                dst = wb[:, bi, 0:2048].rearrange("p (g n) -> p g n", g=8)
                cx.dma("pool", dst, src, reads=[R_s5w], writes=[R_wb[bi]])
            elif kind == "s5we":
                _, ps_, i = item
                g0 = ps_ * 32 + i * 3
                ng = min(3, ps_ * 32 + 32 - g0)
                src = s5w_d[g0:g0 + ng, :, 256:896].rearrange("g p n -> p g n")
                dst = wb[:, bi, 0:ng * 640].rearrange("p (g n) -> p g n", g=ng)
                cx.dma("pool", dst, src, reads=[R_s5w], writes=[R_wb[bi]])
            elif kind == "nqk":
                _, hp = item
                for wi in range(2):
                    src = nqkv_d.rearrange("(kc p) n -> p kc n", p=128)[:, :, wi * D + hp * 128: wi * D + (hp + 1) * 128]
                    dst = wb[:, bi, wi * 1024:(wi + 1) * 1024].rearrange("p (kc n) -> p kc n", kc=KC)
                    cx.dma("pool", dst, src, writes=[R_wb[bi]])
            elif kind == "nv":
                _, hp = item
                src = nqkv_d.rearrange("(kc p) n -> p kc n", p=128)[:, :, 2 * D + hp * 128: 2 * D + (hp + 1) * 128]
                dst = wb[:, bi, 0:1024].rearrange("p (kc n) -> p kc n", kc=KC)
                cx.dma("pool", dst, src, writes=[R_wb[bi]])
            elif kind == "nbA":
                _, hd = item
                cx.dma("pool", wb[:, bi, 0:1152].rearrange("p (t n) -> p t n", t=9), btab_d[hd, :, 0:9, :],
                       writes=[R_wb[bi]], max_dma_last_dim=4096)
            elif kind == "nbB":
                _, hd = item
                cx.dma("pool", wb[:, bi, 0:1536].rearrange("p (t n) -> p t n", t=12), btab_d[hd, :, 9:21, :],
                       writes=[R_wb[bi]], max_dma_last_dim=4096)
            elif kind == "nwo":
                _, hp = item
                cx.dma("pool", wb[:, bi, 0:1024], nwo_d[hp * 128:(hp + 1) * 128, :], writes=[R_wb[bi]])
            elif kind == "glu":
                _, m = item
                for wi, wsrc in enumerate((s5w1_d, s5w2_d)):
                    src = wsrc.rearrange("(kc p) n -> p kc n", p=128)[:, :, m * 128:(m + 1) * 128]
                    dst = wb[:, bi, wi * 1024:(wi + 1) * 1024].rearrange("p (kc n) -> p kc n", kc=KC)
                    cx.dma("pool", dst, src, writes=[R_wb[bi]])
            elif kind == "pool":
                _, s = item
                src = pool_w[s].rearrange("g (kk p) n -> p g kk n", p=128)
                dst = wb[:, bi, :].rearrange("p (g kk n) -> p g kk n", g=4, kk=2)
                cx.dma("pool", dst, src, writes=[R_wb[bi]])
            ws_state["issued"] += 1

        def ws_next(expect):
            i = ws_state["next"]
            assert plan[i][0] == expect[0] and tuple(plan[i][1:]) == tuple(expect[1:]), (plan[i], expect)
            while ws_state["issued"] <= i:
                ws_issue()
            ws_state["next"] += 1
            return i % NWB

        def ws_done():
            while ws_state["issued"] < min(len(plan), ws_state["next"] + NWB - 1):
                ws_issue()

        for _ in range(NWB - 1):
            ws_issue()

        TILES512 = [(t0, min(512, NTOK - t0)) for t0 in range(0, NTOK, 512)]

        def ucol(t):
            return LAT0 + t if t < SEQ else CTX0 + (t - SEQ)

        def norm_phase(l, which, b, ctx_too, out_bf16=True):
            shoff = 0 if which == 0 else 24
            tiles = [t for t in TILES512 if ctx_too or t[0] < SEQ]
            cx.barrier()
            sqb = scr[:, 0:2].rearrange("p a n -> p (a n)").bitcast(BF16).rearrange("p (a n) -> p a n", a=4)
            R_sq = [cx.res(f"sq{i}") for i in range(4)]
            R_tm = [R_scr[2], R_scr[3]]
            cnt = {"sq": 0, "tm": 0}

            def stage_a(ti, t0, n):
                pb = 2 + ti % 2
                ri = ti % 2
                for k in range(KC):
                    si = cnt["sq"] % 4
                    cnt["sq"] += 1
                    cx.op("act", lambda e: e.activation(out=sqb[:, si, 0:n], in_=h[:, k, t0:t0 + n], func=AF.Square),
                          reads=[R_h[k]], writes=[R_sq[si]])
                    e = cx.E["pe"]
                    cx._deps(e, [R_sq[si], R_onesb], [R_ps[pb]] if k == 0 else [])
                    inst = e.h.matmul(pst[pb][:, 0:n], onesb[:], sqb[:, si, 0:n], start=(k == 0), stop=(k == KC - 1))
                    e.count += 1
                    inst.then_inc(e.sem, 1)
                    cx._register((e.sem, e.count), [R_sq[si], R_onesb], [R_ps[pb]])
                cx.op("act", lambda e: e.activation(out=rs[:, ri, 0:n], in_=pst[pb][:, 0:n], func=AF.Ln,
                                                    bias=V("epsv"), scale=1.0 / D),
                      reads=[R_ps[pb], R_vecs], writes=[R_rs[ri]])
                cx.op("act", lambda e: e.activation(out=rs[:, ri, 0:n], in_=rs[:, ri, 0:n], func=AF.Exp, scale=-0.5),
                      reads=[R_rs[ri]], writes=[R_rs[ri]])

            def stage_b(ti, t0, n):
                j = b if t0 < SEQ else 4
                ri = ti % 2
                c0 = ucol(t0)
                for k in range(KC):
                    si = 2 + cnt["tm"] % 2
                    cnt["tm"] += 1
                    cx.op("dve", lambda e: e.scalar_tensor_tensor(
                        out=scr[:, si, 0:n], in0=h[:, k, t0:t0 + n], scalar=gm[:, l, which, k, j:j + 1],
                        in1=rs[:, ri, 0:n], op0=ALU.mult, op1=ALU.mult),
                        reads=[R_h[k], R_gm, R_rs[ri]], writes=[R_scr[si]])
                    if False:
                        pass
                    else:
                        cx.op("pool", lambda e: e.tensor_scalar(
                            out=u[:, k, c0:c0 + n], in0=scr[:, si, 0:n], scalar1=1.0,
                            scalar2=mod[:, l, shoff + k, j:j + 1], op0=ALU.mult, op1=ALU.add),
                            reads=[R_scr[si], R_mod], writes=[R_u[k]])

            stage_a(0, *tiles[0])
            for ti in range(len(tiles)):
                if ti + 1 < len(tiles):
                    stage_a(ti + 1, *tiles[ti + 1])
                stage_b(ti, *tiles[ti])
            cx.barrier()

        def pool_phase(l, b, ctx_too):
            s = l // 3
            bi = ws_next(("pool", s))
            pw = wb[:, bi, :].rearrange("p (g kk n) -> p g kk n", g=4, kk=2)
            psoff = vec_cols["ps"][0] + s * KC
            pboff = vec_cols["pb"][0] + s * KC
            for ji, j in enumerate((b, 4)):
                cx.op("dve", lambda e, ji=ji, j=j: e.tensor_tensor(
                    out=sv[:, ji * 16:ji * 16 + 8], in0=vecs[:, psoff:psoff + KC], in1=mod[:, l, 16:24, j],
                    op=ALU.mult), reads=[R_vecs, R_mod], writes=[R_sv])
                cx.op("dve", lambda e, ji=ji: e.tensor_tensor(
                    out=sv[:, ji * 16 + 8:ji * 16 + 16], in0=vecs[:, pboff:pboff + KC],
                    in1=sv[:, ji * 16:ji * 16 + 8], op=ALU.mult), reads=[R_vecs, R_sv], writes=[R_sv])
            abuf = big[:, 0:2 * 2 * UCOLS].bitcast(F32).rearrange("p (a c) -> p a c", a=2)
            seqs = [(LAT0, SEQ, 0)] + ([(CTX0, CTX, SEQ)] if ctx_too else [])
            toff = vec_cols["ptbl"][0]
            for k in range(KC):
                wi = k // 2
                w = POOL_WINDOWS[wi]
                src = u[:, k, :]
                width = UCOLS
                step = 1
                ai = 0
                first = True
                while step < w:
                    width -= step
                    dst = abuf[:, ai, 0:width]
                    s0 = src[:, 0:width]
                    s1 = src[:, step:step + width]
                    cx.op("dve", lambda e, dst=dst, s0=s0, s1=s1: e.tensor_tensor(out=dst, in0=s0, in1=s1, op=ALU.add),
                          reads=[R_u[k], R_big[1]], writes=[R_big[1]])
                    src = abuf[:, ai, :]
                    ai ^= 1
                    step *= 2
                    first = False
                aw = src
                for (base, n, tb) in seqs:
                    cx.op("dve", lambda e, base=base, n=n, tb=tb, aw=aw: e.scalar_tensor_tensor(
                        out=u[:, k, base + w // 2:base + n - (w // 2 - 1)],
                        in0=aw[:, base: base + n - w + 1], scalar=1.0 / w,
                        in1=u[:, k, base + w // 2:base + n - (w // 2 - 1)], op0=ALU.mult, op1=ALU.subtract),
                        reads=[R_big[1], R_u[k]], writes=[R_u[k]])
                    nl_ = w // 2
                    tl = toff + wi * 16
                    cx.op("dve", lambda e, base=base, aw=aw, nl_=nl_, tl=tl: e.tensor_tensor(
                        out=scr[:, 0, 0:nl_], in0=aw[:, base - w // 2: base - w // 2 + nl_],
                        in1=vecs[:, tl:tl + nl_], op=ALU.mult), reads=[R_big[1], R_vecs], writes=[R_scr[0]])
                    cx.op("dve", lambda e, base=base, nl_=nl_, tb=tb: e.tensor_tensor(
                        out=u[:, k, base:base + nl_], in0=scr[:, 0, 0:nl_], in1=u[:, k, base:base + nl_],
                        op=ALU.subtract), reads=[R_scr[0], R_u[k]], writes=[R_u[k]])
                    nr = w // 2 - 1
                    if nr > 0:
                        t1 = n - w // 2 + 1
                        cx.op("dve", lambda e, base=base, aw=aw, nr=nr, tl=tl, t1=t1: e.tensor_tensor(
                            out=scr[:, 0, 0:nr], in0=aw[:, base + t1 - w // 2: base + t1 - w // 2 + nr],
                            in1=vecs[:, tl + 8:tl + 8 + nr], op=ALU.mult),
                            reads=[R_big[1], R_vecs], writes=[R_scr[0]])
                        cx.op("dve", lambda e, base=base, nr=nr, tb=tb, t1=t1: e.tensor_tensor(
                            out=u[:, k, base + t1:base + t1 + nr], in0=scr[:, 0, 0:nr],
                            in1=u[:, k, base + t1:base + t1 + nr], op=ALU.subtract),
                            reads=[R_scr[0], R_u[k]], writes=[R_u[k]])
            tiles = [t for t in TILES512 if ctx_too or t[0] < SEQ]
            cnt = 0
            for gi in range(4):
                for mi in range(2):
                    m = 2 * gi + mi
                    for (t0, n) in tiles:
                        ji = 0 if t0 < SEQ else 1
                        pb = 2 + cnt % 2
                        si = cnt % 2
                        cnt += 1
                        c0 = ucol(t0)
                        mms = [(pst[pb][:, 0:n], pw[:, gi, kk, mi * 128:(mi + 1) * 128], u[:, 2 * gi + kk, c0:c0 + n])
                               for kk in range(2)]
                        cx.mm_group(mms, reads=[R_wb[bi], R_u[2 * gi], R_u[2 * gi + 1]], writes=[R_ps[pb]])
                        cx.op("act", lambda e, pb=pb, si=si, m=m, ji=ji, n=n: e.activation(
                            out=scr[:, si, 0:n], in_=pst[pb][:, 0:n], func=AF.Identity,
                            bias=sv[:, ji * 16 + 8 + m: ji * 16 + 9 + m], scale=sv[:, ji * 16 + m: ji * 16 + m + 1]),
                            reads=[R_ps[pb], R_sv], writes=[R_scr[si]])
                        cx.op("dve", lambda e, si=si, m=m, t0=t0, n=n: e.tensor_tensor(
                            out=h[:, m, t0:t0 + n], in0=h[:, m, t0:t0 + n], in1=scr[:, si, 0:n], op=ALU.add),
                            reads=[R_scr[si], R_h[m]], writes=[R_h[m]])
            ws_done()

        def ffn_phase(l, b, ctx_too):
            cwoff = vec_cols["cw"][0] + l * 3 * 44
            cboff = vec_cols["cb"][0] + l * 44
            up_tiles = []
            for i in range(5):
                t0 = 410 * i
                n = min(410, SEQ - t0)
                up_tiles.append((LAT0 + t0 - 1, n, t0))
            if ctx_too:
                up_tiles.append((CTX0 - 1, CTX, SEQ))
            dn_tiles = [t for t in TILES512 if ctx_too or t[0] < SEQ]
            ecnt = 0
            dcnt = 0
            pend = []
            cx.barrier()
            NSET = 4
            ftmp = big[:, 6 * NTOK:6 * NTOK + NSET * 2 * 416 * 2].bitcast(F32).rearrange("p (a n) -> p a n", a=NSET * 2)
            R_ft = [cx.res(f"ft{i}") for i in range(NSET * 2)]
            for g in range(4):
                prs = FFN_GROUPS[g]
                hb = 0
                hid = big[:, 0:6 * NTOK].rearrange("p (a t) -> p a t", a=6)
                for jj, j in enumerate(prs):
                    bi = ws_next(("wup", l, j))
                    wt = wb[:, bi, :].rearrange("p (kc n) -> p kc n", kc=KC)
                    for (c0, n, t0) in up_tiles:
                        nin = n + 2
                        pv = 2 + 2 * (ecnt % 3)
                        pg = pv + 1
                        sV = 2 * (ecnt % NSET)
                        sG = sV + 1
                        ecnt += 1
                        for half, pbk in ((0, pv), (1, pg)):
                            mms = [(pst[pbk][:, 0:nin], wt[:, kc, half * 128:(half + 1) * 128], u[:, kc, c0:c0 + nin])
                                   for kc in range(KC)]
                            cx.mm_group(mms, reads=[R_wb[bi]] + R_u, writes=[R_ps[pbk]])
                        for half, pbk, si in ((0, pv, sV), (1, pg, sG)):
                            ch = half * NFC + j
                            w0 = vecs[:, cwoff + 0 * 44 + ch: cwoff + 0 * 44 + ch + 1]
                            w1 = vecs[:, cwoff + 1 * 44 + ch: cwoff + 1 * 44 + ch + 1]
                            w2 = vecs[:, cwoff + 2 * 44 + ch: cwoff + 2 * 44 + ch + 1]
                            bb = vecs[:, cboff + ch: cboff + ch + 1]
                            cx.op("act", lambda e, pbk=pbk, si=si, w1=w1, bb=bb, n=n: e.activation(
                                out=ftmp[:, si, 0:n], in_=pst[pbk][:, 1:1 + n], func=AF.Identity, bias=bb, scale=w1),
                                reads=[R_ps[pbk], R_vecs], writes=[R_ft[si]])
                            cx.op("dve", lambda e, pbk=pbk, si=si, w0=w0, n=n: e.scalar_tensor_tensor(
                                out=ftmp[:, si, 0:n], in0=pst[pbk][:, 0:n], scalar=w0, in1=ftmp[:, si, 0:n],
                                op0=ALU.mult, op1=ALU.add), reads=[R_ps[pbk], R_vecs, R_ft[si]], writes=[R_ft[si]])
                            cx.op("dve", lambda e, pbk=pbk, si=si, w2=w2, n=n: e.scalar_tensor_tensor(
                                out=ftmp[:, si, 0:n], in0=pst[pbk][:, 2:2 + n], scalar=w2, in1=ftmp[:, si, 0:n],
                                op0=ALU.mult, op1=ALU.add), reads=[R_ps[pbk], R_vecs, R_ft[si]], writes=[R_ft[si]])
                        def tail(n=n, jj=jj, t0=t0, sG=sG, sV=sV):
                            cx.op("act", lambda e: e.activation(out=ftmp[:, sG, 0:n], in_=ftmp[:, sG, 0:n], func=AF.Silu),
                                  reads=[R_ft[sG]], writes=[R_ft[sG]])
                            cx.op("pool", lambda e: e.tensor_tensor(
                                out=hid[:, jj, t0:t0 + n], in0=ftmp[:, sG, 0:n], in1=ftmp[:, sV, 0:n], op=ALU.mult),
                                reads=[R_ft[sG], R_ft[sV]], writes=[R_big[hb]])
                        if pend:
                            pend.pop()()
                        pend.append(tail)
                    ws_done()
                if pend:
                    pend.pop()()
                wts = []
                for jj in range(0, len(prs), 2):
                    bi = ws_next(("wdn", l, prs[jj:jj + 2]))
                    wts.append(bi)
                for c in range(KC):
                    for (t0, n) in dn_tiles:
                        j = b if t0 < SEQ else 4
                        pb = dcnt % 2
                        dcnt += 1
                        mms = []
                        for jj in range(len(prs)):
                            bi = wts[jj // 2]
                            wt = wb[:, bi, :].rearrange("p (a d) -> p a d", a=2)
                            mms.append((pst[pb][:, 0:n], wt[:, jj % 2, c * 128:(c + 1) * 128], hid[:, jj, t0:t0 + n]))
                        cx.mm_group(mms, reads=[R_wb[x] for x in wts] + [R_big[hb]], writes=[R_ps[pb]])
                        cx.op("dve", lambda e, pb=pb, c=c, t0=t0, n=n, j=j: e.scalar_tensor_tensor(
                            out=h[:, c, t0:t0 + n], in0=pst[pb][:, 0:n], scalar=mod[:, l, 40 + c, j:j + 1],
                            in1=h[:, c, t0:t0 + n], op0=ALU.mult, op1=ALU.add),
                            reads=[R_ps[pb], R_mod, R_h[c]], writes=[R_h[c]])
                ws_done()
            cx.barrier()

        prot = {"i": 0}

        def rot():
            prot["i"] = (prot["i"] + 1) % 8
            return prot["i"]

        def s5_phase(l, b):
            GP = 32
            Xb = big[:, 0:GP * 288].rearrange("p (g j) -> p g j", g=GP)
            Sh = big[:, GP * 288:GP * 288 + 2 * GP * 289].rearrange("p (r g j) -> p r g j", r=2, g=GP)
            R_Xg = [cx.res(f"s5x{i}") for i in range(GP)]
            R_us = [[cx.res(f"s5u{k}_{s_}") for s_ in range(8)] for k in range(KC)]
            R_S = R_big[1]
            cx.barrier()
            Hs, T1, T2 = hst[:, 0], hst[:, 1], hst[:, 2]
            doff = vec_cols["s5d"][0]
            for ps_ in range(2 if S5DBG >= 1 else 0):
                cx.op("pool", lambda e: e.memset(Sh[:, :, :, 0:1], 0.0), writes=[R_S])
                cx.op("dve", lambda e: e.memset(Hs, 0.0), writes=[R_hst[0]])
                for gi in range(GP):
                    g = ps_ * GP + gi
                    kc, gl = g // 8, g % 8
                    q, par = gl // 2, gl % 2
                    rows, so = (slice(32 * q, 32 * q + 32), 0) if q < 3 else (slice(64, 128), 32)
                    pb = rot()
                    groups = [
                        [(pst[pb][:, 0:256], sel[rows, so + par * 8 + s_, :], u[rows, kc, LAT0 + s_:LAT0 + s_ + SEQ:8])
                         for s_ in range(8)],
                        [(pst[pb][:, 256:288], sel[rows, so + par * 8 + s_, :], u[rows, kc, CTX0 + s_:CTX0 + s_ + CTX:8])
                         for s_ in range(8)],
                    ]
                    cx.mm_multi(groups, reads=[R_sel, R_u[kc]], writes=[R_ps[pb]])
                    cx.op("act", lambda e: e.activation(out=Xb[:, gi, :], in_=pst[pb][:, 0:288], func=AF.Identity),
                          reads=[R_ps[pb]], writes=[R_Xg[gi]])
                for gi in range(GP):
                    if gi % 8 == 0:
                        if gi > 0:
                            ws_done()
                        bi = ws_next(("s5wb", ps_, gi // 8))
                        wv = wb[:, bi, 0:2048].rearrange("p (a n) -> p a n", a=8)
                    W = wv[:, gi % 8]
                    for ri in range(2 if S5DBG >= 2 else 0):
                        pb2 = rot()
                        wf = W[:, ri * 128:ri * 128 + 64]
                        wr = W[:, ri * 128 + 64:ri * 128 + 128]
                        groups = [
                            [(pst[pb2][0:64, 0:32], wf, Xb[:, gi, 256:288])],
                            [(pst[pb2][0:64, 32:288], wf, Xb[:, gi, 0:256])],
                            [(pst[pb2][64:128, 0:288], wr, Xb[:, gi, ::-1])],
                        ]
                        cx.mm_multi(groups, reads=[R_wb[bi], R_Xg[gi]], writes=[R_ps[pb2]])
                        cx.op("dve", lambda e: e.tensor_copy(out=Sh[:, ri, gi, 1:289], in_=pst[pb2][:, 0:288]),
                              reads=[R_ps[pb2]], writes=[R_S])
                ws_done()
                gsl = slice(ps_ * GP, (ps_ + 1) * GP)
                a16 = rs[:].rearrange("p a n -> p (a n)")[:, 0:256].rearrange("p (c r g) -> p c r g", c=2, r=2)
                if ps_ == 0:
                    cx.dma("sp", rs[:].rearrange("p a n -> p (a n)")[:, 0:256], a16_d[:, :], reads=[R_a16d],
                           writes=[R_rs[0]])
                mtab2 = a16[:, :, :, gsl]
                Pm = hst[:, 1:3]
                hsb = Hs.unsqueeze(1).to_broadcast([128, 2, 2, GP])
                scrf = scr[:].rearrange("p a n -> p (a n)")

                def pair_fix(dst_lo, src_lo):
                    for gl0 in range(0, GP, 3):
                        ng = min(3, GP - gl0)
                        T1 = scrf[:, 0:2 * ng * 144].rearrange("p (r g m) -> p r g m", r=2, g=ng)
                        T2 = scrf[:, 1024:1024 + 2 * ng * 144].rearrange("p (r g m) -> p r g m", r=2, g=ng)
                        src = Sh[:, :, gl0:gl0 + ng, src_lo:src_lo + 287:2]
                        dst = Sh[:, :, gl0:gl0 + ng, dst_lo:dst_lo + 287:2]
                        g0 = ps_ * GP + gl0
                        ar_ = a8m[:, 0, :, g0:g0 + ng].unsqueeze(3).to_broadcast([128, 2, ng, 144])
                        ai_ = a8m[:, 1, ::-1, g0:g0 + ng].unsqueeze(3).to_broadcast([128, 2, ng, 144])
                        cx.op("dve", lambda e: e.tensor_tensor(out=T1, in0=src, in1=ar_, op=ALU.mult),
                              reads=[R_S, R_a8], writes=[R_scr[0], R_scr[1]])
                        cx.op("pool", lambda e: e.tensor_tensor(out=T2, in0=src[:, ::-1], in1=ai_, op=ALU.mult),
                              reads=[R_S, R_a8], writes=[R_scr[2], R_scr[3]])
                        cx.op("dve", lambda e: e.tensor_tensor(out=T1, in0=T1, in1=T2, op=ALU.add),
                              reads=[R_scr[0], R_scr[1], R_scr[2], R_scr[3]], writes=[R_scr[0], R_scr[1]])
                        cx.op("dve", lambda e: e.tensor_tensor(out=dst, in0=dst, in1=T1, op=ALU.add),
                              reads=[R_scr[0], R_scr[1], R_S], writes=[R_S])

                if S5DBG >= 3:
                    pair_fix(2, 1)
                    for m in range(144):
                        col = 2 * m + 2
                        cx.op("dve", lambda e: e.tensor_tensor(out=Pm, in0=hsb, in1=mtab2, op=ALU.mult),
                              reads=[R_hst[0], R_rs[0]], writes=[R_hst[1]])
                        cx.op("dve", lambda e: e.tensor_tensor(out=Pm[:, 0], in0=Pm[:, 0], in1=Pm[:, 1, ::-1, :], op=ALU.add),
                              reads=[R_hst[1]], writes=[R_hst[1]])
                        cx.op("dve", lambda e: e.tensor_tensor(out=Hs, in0=Pm[:, 0], in1=Sh[:, :, :, col], op=ALU.add),
                              reads=[R_hst[1], R_S], writes=[R_hst[0]])
                        cx.op("act", lambda e: e.activation(out=Sh[:, :, :, col], in_=Hs, func=AF.Identity),
                              reads=[R_hst[0]], writes=[R_S])
                    pair_fix(1, 0)
                for gi in range(GP if S5DBG >= 4 else 0):
                    if gi % 3 == 0:
                        if gi > 0:
                            ws_done()
                        bi = ws_next(("s5we", ps_, gi // 3))
                        wv = wb[:, bi, 0:1920].rearrange("p (a n) -> p a n", a=3)
                    W = wv[:, gi % 3]
                    pb = rot()
                    mms = [(pst[pb][:, 0:288], W[:, 0:128], Xb[:, gi, 0:288])]
                    for ri in range(2):
                        wcf = W[:, 128 + ri * 128:128 + (ri + 1) * 128]
                        mms.append((pst[pb][:, 256:288], wcf, Sh[:, ri, gi, 0:32]))
                        mms.append((pst[pb][:, 0:256], wcf, Sh[:, ri, gi, 32:288]))
                    for ri in range(2):
                        wcr = W[:, 384 + ri * 128:384 + (ri + 1) * 128]
                        mms.append((pst[pb][:, 0:288], wcr, Sh[:, ri, gi, 287::-1]))
                    cx.mm_group(mms, reads=[R_wb[bi], R_Xg[gi], R_S], writes=[R_ps[pb]])
                    cx.op("act", lambda e: e.activation(out=Xb[:, gi, :], in_=pst[pb][:, 0:288], func=AF.Identity),
                          reads=[R_ps[pb]], writes=[R_Xg[gi]])
                ws_done()
                for kcl in range(4 if S5DBG >= 5 else 0):
                    kc = ps_ * 4 + kcl
                    for s_ in range(8):
                        q, par = s_ // 2, s_ % 2
                        rows, so = (slice(32 * q, 32 * q + 32), 0) if q < 3 else (slice(64, 128), 32)
                        pb = rot()
                        mms = [(pst[pb][:, 0:288], sel[rows, so + 16 + gl * 2 + par, :], Xb[rows, kcl * 8 + gl, 0:288])
                               for gl in range(8)]
                        cx.mm_group(mms, reads=[R_sel] + [R_Xg[kcl * 8 + gl] for gl in range(8)], writes=[R_ps[pb]])
                        for (c0, n, p0) in ((LAT0 + s_, SEQ // 8, 0), (CTX0 + s_, CTX // 8, 256)):
                            si = rot() % 4
                            uv = u[:, kc, c0:c0 + 8 * n:8]
                            cx.op("dve", lambda e: e.scalar_tensor_tensor(
                                out=scr[:, si, 0:n], in0=uv, scalar=vecs[:, doff + kc:doff + kc + 1],
                                in1=pst[pb][:, p0:p0 + n], op0=ALU.mult, op1=ALU.add),
                                reads=[R_us[kc][s_], R_vecs, R_ps[pb]], writes=[R_scr[si]])
                            cx.op("act", lambda e: e.activation(out=uv, in_=scr[:, si, 0:n], func=AF.Gelu_apprx_tanh),
                                  reads=[R_scr[si]], writes=[R_us[kc][s_]])
            cx.barrier()
            b1off = vec_cols["s5b1"][0]
            b2off = vec_cols["s5b2"][0]
            for m in range(KC if S5DBG >= 6 else 0):
                bi = ws_next(("glu", m))
                wv = wb[:, bi, :].rearrange("p (w kc n) -> p w kc n", w=2, kc=KC)
                for (t0, n) in TILES512:
                    j = b if t0 < SEQ else 4
                    c0 = ucol(t0)
                    pa, pbk = rot(), rot()
                    for wi, pk in ((0, pa), (1, pbk)):
                        mms = [(pst[pk][:, 0:n], wv[:, wi, kc, :], u[:, kc, c0:c0 + n]) for kc in range(KC)]
                        cx.mm_group(mms, reads=[R_wb[bi]] + R_u, writes=[R_ps[pk]])
                    si = rot() % 4
                    sj = (si + 1) % 4
                    cx.op("act", lambda e: e.activation(out=scr[:, si, 0:n], in_=pst[pbk][:, 0:n], func=AF.Sigmoid,
                                                        bias=vecs[:, b2off + m:b2off + m + 1], scale=1.0),
                          reads=[R_ps[pbk], R_vecs], writes=[R_scr[si]])
                    cx.op("dve", lambda e: e.scalar_tensor_tensor(
                        out=scr[:, sj, 0:n], in0=pst[pa][:, 0:n], scalar=vecs[:, b1off + m:b1off + m + 1],
                        in1=scr[:, si, 0:n], op0=ALU.add, op1=ALU.mult),
                        reads=[R_ps[pa], R_vecs, R_scr[si]], writes=[R_scr[sj]])
                    cx.op("dve", lambda e: e.scalar_tensor_tensor(
                        out=h[:, m, t0:t0 + n], in0=scr[:, sj, 0:n], scalar=mod[:, l, 16 + m, j:j + 1],
                        in1=h[:, m, t0:t0 + n], op0=ALU.mult, op1=ALU.add),
                        reads=[R_scr[sj], R_mod, R_h[m]], writes=[R_h[m]])
                ws_done()

        def nat_tiles(qb):
            if 2 <= qb <= 13:
                return [(qb - 2 + i, i) for i in range(5)]
            base = {0: 5, 1: 9, 14: 13, 15: 17}[qb]
            k0 = 0 if qb < 2 else 12
            return [(k0 + i, base + i) for i in range(4)]

        def nat_phase(l, b):
            cx.barrier()
            pm0 = V("pm0")
            pm1 = V("pm1")
            NQ = 2048 + 2 * 2304 + 18 * 2 * 65
            R_qkv = [cx.res("nqkv0"), cx.res("nqkv1")]
            R_ot = cx.res("notok")
            R_oth = [cx.res("noth0"), cx.res("noth1")]
            R_pt = [cx.res("npt0"), cx.res("npt1")]
            o0 = 2 * NQ
            OTK = big[:, o0:o0 + 2048].rearrange("p (q h d) -> p q h d", q=16, h=2)
            OTH = big[:, o0 + 2048:o0 + 2048 + 4096].rearrange("p (a t) -> p a t", a=2)
            PT = big[:, o0 + 6144:o0 + 6144 + 1792].rearrange("p (a t n) -> p a t n", a=2, t=7)
            assert o0 + 6144 + 1792 <= BIGE
            pcnt = 0
            npend = []
            for hp in range(8):
                qi = hp % 2
                QT = big[:, qi * NQ:qi * NQ + 2048]
                KZ = big[:, qi * NQ + 2048:qi * NQ + 2048 + 4608].rearrange("p (z t) -> p z t", z=2)
                VA = big[:, qi * NQ + 6656:qi * NQ + 6656 + 2340].rearrange("p (t h d) -> p t h d", t=18, h=2)
                RQ = R_qkv[qi]
                cx.op("pool", lambda e: e.memset(VA[:, :, :, 64:65], 1.0), writes=[RQ])
                bi = ws_next(("nqk", hp))
                wqk = wb[:, bi, :].rearrange("p (w kc n) -> p w kc n", w=2, kc=KC)
                for (t0, n) in TILES512:
                    c0 = ucol(t0)
                    if t0 < SEQ:
                        pb = rot()
                        cx.mm_group([(pst[pb][:, 0:n], wqk[:, 0, kc, :], u[:, kc, c0:c0 + n]) for kc in range(KC)],
                                    reads=[R_wb[bi]] + R_u, writes=[R_ps[pb]])
                        cx.op("act", lambda e: e.activation(out=QT[:, t0:t0 + n], in_=pst[pb][:, 0:n], func=AF.Identity,
                                                            scale=0.125), reads=[R_ps[pb]], writes=[RQ])
                    pb = rot()
                    cx.mm_group([(pst[pb][:, 0:n], wqk[:, 1, kc, :], u[:, kc, c0:c0 + n]) for kc in range(KC)],
                                reads=[R_wb[bi]] + R_u, writes=[R_ps[pb]])
                    cx.op("act", lambda e: e.activation(out=KZ[:, 0, t0:t0 + n], in_=pst[pb][:, 0:n], func=AF.Identity,
                                                        scale=pm0), reads=[R_ps[pb], R_vecs], writes=[RQ])
                    cx.op("dve", lambda e: e.tensor_scalar(out=KZ[:, 1, t0:t0 + n], in0=pst[pb][:, 0:n], scalar1=pm1,
                                                           scalar2=None, op0=ALU.mult),
                          reads=[R_ps[pb], R_vecs], writes=[RQ])
                ws_done()
                bi = ws_next(("nv", hp))
                wv_ = wb[:, bi, 0:1024].rearrange("p (kc n) -> p kc n", kc=KC)
                for t4 in range(0, 18, 4):
                    nt = min(4, 18 - t4)
                    pb = rot()
                    groups = []
                    for tt in range(t4, t4 + nt):
                        c0 = ucol(128 * tt)
                        groups.append([(pst[pb][:, (tt - t4) * 128:(tt - t4 + 1) * 128], u[:, kc, c0:c0 + 128], wv_[:, kc, :])
                                       for kc in range(KC)])
                    cx.mm_multi(groups, reads=[R_wb[bi]] + R_u, writes=[R_ps[pb]])
                    cx.op("dve", lambda e: e.tensor_copy(
                        out=VA[:, t4:t4 + nt, :, 0:64],
                        in_=pst[pb][:, 0:nt * 128].rearrange("p (t h d) -> p t h d", t=nt, h=2)),
                        reads=[R_ps[pb]], writes=[RQ])
                ws_done()
                for hh in range(2):
                    hd = 2 * hp + hh
                    biA = ws_next(("nbA", hd))
                    biB = ws_next(("nbB", hd))
                    tA = wb[:, biA, 0:1152].rearrange("p (t n) -> p t n", t=9)
                    tB = wb[:, biB, 0:1536].rearrange("p (t n) -> p t n", t=12)
                    for qb in range(16):
                        loc = nat_tiles(qb)
                        tiles = loc + [(16, None), (17, None)]
                        ntl = len(tiles)
                        pa, pbk = rot(), rot()
                        pi_ = pcnt % 2
                        pcnt += 1
                        nloc = len(loc)
                        tb0 = loc[0][1]
                        tsrc, tres, tofs = (tA, R_wb[biA], tb0) if tb0 < 9 else (tB, R_wb[biB], tb0 - 9)
                        for (bank, lo, hi) in ((pa, 0, 4), (pbk, 4, ntl)):
                            groups = []
                            for ti in range(lo, hi):
                                kt, tb = tiles[ti]
                                o_ = pst[bank][:, (ti - lo) * 128:(ti - lo + 1) * 128]
                                grp = [(o_, KZ[:, hh, kt * 128:(kt + 1) * 128], QT[:, qb * 128:(qb + 1) * 128])]
                                if lo > 0 and tb is not None:
                                    grp.append((o_, identb[:], tsrc[:, tofs + ti, :]))
                                groups.append(grp)
                            cx.mm_multi(groups, reads=[RQ, tres, R_identb], writes=[R_ps[bank]])
                            nb_ = (min(hi, nloc) - lo) if lo == 0 else 0
                            if nb_ > 0:
                                cx.op("dve", lambda e: e.tensor_tensor(
                                    out=pst[bank][:, 0:nb_ * 128].rearrange("p (t n) -> p t n", n=128),
                                    in0=pst[bank][:, 0:nb_ * 128].rearrange("p (t n) -> p t n", n=128),
                                    in1=tsrc[:, tofs + lo:tofs + lo + nb_, :], op=ALU.add),
                                    reads=[R_ps[bank], tres], writes=[R_ps[bank]])
                            cx.op("act", lambda e: e.activation(
                                out=PT[:, pi_, lo:hi, :], in_=pst[bank][:, 0:(hi - lo) * 128].rearrange("p (t n) -> p t n", n=128),
                                func=AF.Exp), reads=[R_ps[bank]], writes=[R_pt[pi_]])
                        def tail(pi_=pi_, tiles=tiles, ntl=ntl, qb=qb, hh=hh, ri_=pcnt % 4, VA=VA, RQ=RQ):
                            po = rot()
                            cx.mm_group([(pst[po][:, 0:65], PT[:, pi_, ti, :], VA[:, tiles[ti][0], hh, :]) for ti in range(ntl)],
                                        reads=[R_pt[pi_], RQ], writes=[R_ps[po]])
                            cx.op("dve", lambda e: e.reciprocal(out=rcp[:, ri_:ri_ + 1], in_=pst[po][:, 64:65]),
                                  reads=[R_ps[po]], writes=[R_rcp[ri_]])
                            cx.op("act", lambda e: e.activation(out=OTK[:, qb, hh, :], in_=pst[po][:, 0:64],
                                                                func=AF.Identity, scale=rcp[:, ri_:ri_ + 1]),
                                  reads=[R_ps[po], R_rcp[ri_]], writes=[R_ot])
                        if npend:
                            npend.pop()()
                        npend.append(tail)
                    ws_done()
                if npend:
                    npend.pop()()
                oi = hp % 2
                ptb = es.enter_context(nc.psum_tensor(f"ptb{hp}_{b}", [128, 1024], BF16)) if False else None
                for q4 in range(0, 16, 4):
                    pb = rot()
                    pv = pst[pb][:].bitcast(BF16)
                    cx.pe_raw([lambda e, qq=qq: e.transpose(pv[:, (qq - q4) * 128:(qq - q4 + 1) * 128],
                                                             OTK[:, qq].rearrange("p h d -> p (h d)"), identb[:])
                               for qq in range(q4, q4 + 4)], reads=[R_ot, R_identb], writes=[R_ps[pb]])
                    cx.op("act", lambda e: e.activation(out=OTH[:, oi, q4 * 128:(q4 + 4) * 128], in_=pv[:, 0:512],
                                                        func=AF.Identity), reads=[R_ps[pb]], writes=[R_oth[oi]])
                bi = ws_next(("nwo", hp))
                for m in range(KC):
                    for (t0, n) in TILES512:
                        if t0 >= SEQ:
                            continue
                        pb = rot()
                        cx.mm_group([(pst[pb][:, 0:n], wb[:, bi, m * 128:(m + 1) * 128], OTH[:, oi, t0:t0 + n])],
                                    reads=[R_wb[bi], R_oth[oi]], writes=[R_ps[pb]])
                        cx.op("dve", lambda e: e.scalar_tensor_tensor(
                            out=h[:, m, t0:t0 + n], in0=pst[pb][:, 0:n], scalar=mod[:, l, 16 + m, b:b + 1],
                            in1=h[:, m, t0:t0 + n], op0=ALU.mult, op1=ALU.add),
                            reads=[R_ps[pb], R_mod, R_h[m]], writes=[R_h[m]])
                ws_done()
            cx.barrier()

        def final_phase(b):
            goff = vec_cols["fg"][0]
            tiles = [t for t in TILES512 if t[0] < SEQ]
            cx.barrier()
            sqb = scr[:, 0:2].rearrange("p a n -> p (a n)").bitcast(BF16).rearrange("p (a n) -> p a n", a=4)
            R_sq = [cx.res(f"fsq{i}") for i in range(4)]
            NST = 8
            stg = big[:, 0:NST * 1024].bitcast(F32).rearrange("p (a n) -> p a n", a=NST)
            R_stg = [cx.res(f"fst{i}") for i in range(NST)]
            cnt = {"sq": 0, "st": 0}

            def stage_a(ti, t0, n):
                pb = 2 + ti % 2
                ri = ti % 2
                for k in range(KC):
                    si = cnt["sq"] % 4
                    cnt["sq"] += 1
                    cx.op("act", lambda e: e.activation(out=sqb[:, si, 0:n], in_=h[:, k, t0:t0 + n], func=AF.Square),
                          reads=[R_h[k]], writes=[R_sq[si]])
                    e = cx.E["pe"]
                    cx._deps(e, [R_sq[si], R_onesb], [R_ps[pb]] if k == 0 else [])
                    inst = e.h.matmul(pst[pb][:, 0:n], onesb[:], sqb[:, si, 0:n], start=(k == 0), stop=(k == KC - 1))
                    e.count += 1
                    inst.then_inc(e.sem, 1)
                    cx._register((e.sem, e.count), [R_sq[si], R_onesb], [R_ps[pb]])
                cx.op("act", lambda e: e.activation(out=rs[:, ri, 0:n], in_=pst[pb][:, 0:n], func=AF.Ln,
                                                    bias=V("epsv"), scale=1.0 / D),
                      reads=[R_ps[pb], R_vecs], writes=[R_rs[ri]])
                cx.op("act", lambda e: e.activation(out=rs[:, ri, 0:n], in_=rs[:, ri, 0:n], func=AF.Exp, scale=-0.5),
                      reads=[R_rs[ri]], writes=[R_rs[ri]])

            def stage_b(ti, t0, n):
                ri = ti % 2
                for k in range(KC):
                    si = cnt["st"] % NST
                    cnt["st"] += 1
                    if cfg.final_norm:
                        cx.op("dve", lambda e: e.scalar_tensor_tensor(
                            out=stg[:, si, 0:n], in0=h[:, k, t0:t0 + n], scalar=vecs[:, goff + k:goff + k + 1],
                            in1=rs[:, ri, 0:n], op0=ALU.mult, op1=ALU.mult),
                            reads=[R_h[k], R_vecs, R_rs[ri]], writes=[R_stg[si]])
                    else:
                        cx.op("dve", lambda e: e.tensor_copy(out=stg[:, si, 0:n], in_=h[:, k, t0:t0 + n]),
                              reads=[R_h[k]], writes=[R_stg[si]])
                    cx.dma("sp", yT[b, k * 128:(k + 1) * 128, t0:t0 + n], stg[:, si, 0:n], reads=[R_stg[si]])

            stage_a(0, *tiles[0])
            for ti in range(len(tiles)):
                if ti + 1 < len(tiles):
                    stage_a(ti + 1, *tiles[ti + 1])
                stage_b(ti, *tiles[ti])
            cx.wait_all("sp", R_stg)
            cx.barrier()

        for b in range(NB):
            cx.dma("sp", h[:, :, 0:SEQ], xT[b].rearrange("(k p) t -> p k t", p=128), writes=R_h)
            cx.dma("sp", h[:, :, SEQ:NTOK], cT[b].rearrange("(k p) t -> p k t", p=128), writes=R_h)
            for l in range(cfg.nl):
                kind = l % 3
                ctx_out = any((jj % 3) != 0 for jj in range(l + 1, DEPTH))
                ctx_in = ctx_out or kind != 0
                norm_phase(l, 0, b, ctx_in)
                if kind == 0:
                    pool_phase(l, b, ctx_out)
                elif kind == 1:
                    s5_phase(l, b)
                else:
                    nat_phase(l, b)
                norm_phase(l, 1, b, ctx_out)
                ffn_phase(l, b, ctx_out)
            final_phase(b)
        print("instructions:", cx.n_inst, "sems left:", len(cx.free_sems), "sbuf free:", nc.sbuf_bytes_remaining)
    return nc


FFN_GROUPS = [list(range(0, 6)), list(range(6, 12)), list(range(12, 17)), list(range(17, 22))]


def make_bias_tables(rpb):
    cases = [(2, 0 + i) for i in range(5)]
    cases += [(0, i) for i in range(4)] + [(1, i) for i in range(4)]
    cases += [(14, 12 + i) for i in range(4)] + [(15, 12 + i) for i in range(4)]
    out = np.empty((16, 128, 21, 128), np.float32)
    kk = np.arange(128)
    for t, (qb, kt) in enumerate(cases):
        krow = 2 * kt + kk // 64
        kcol = kk % 64
        qrow = 2 * qb + kk // 64
        qcol = kk % 64
        rs = np.clip(qrow - 4, 0, 24)
        cs = np.clip(qcol - 8, 0, 48)
        rvalid = (krow[:, None] >= rs[None, :]) & (krow[:, None] < rs[None, :] + 8)
        cvalid = (kcol[:, None] >= cs[None, :]) & (kcol[:, None] < cs[None, :] + 16)
        roff = np.clip(krow[:, None] - qrow[None, :] + 7, 0, 14)
        coff = np.clip(kcol[:, None] - qcol[None, :], -15, 15) + 15
        g = rpb[:, roff, coff]
        out[:, :, t, :] = np.where((rvalid & cvalid)[None], g, np.float32(-1e30))
    return out


def prep_shared(inp):
    vp = make_vecs(inp)
    vp.add("epsv", np.full((128, 1), EPS, np.float32))
    pm = np.zeros((128, 2), np.float32)
    pm[0:64, 0] = 1.0
    pm[64:128, 1] = 1.0
    vp.add("pm0", pm[:, 0:1])
    vp.add("pm1", pm[:, 1:2])
    shared = {
        "vecs": vp.pack(),
        "ada_w": np.ascontiguousarray(inp["ada_w"], np.float32),
        "wup": None,
        "wdn": np.ascontiguousarray(inp["ffn_w_down"], np.float32),
        "pool_w": np.ascontiguousarray(inp["pool_w"], np.float32),
        "s5w1": np.ascontiguousarray(inp["s5_w1"][0], np.float32),
        "s5w2": np.ascontiguousarray(inp["s5_w2"][0], np.float32),
    }
    shared["nqkv"] = np.ascontiguousarray(inp["nat_w_qkv"][0], np.float32)
    shared["nwo"] = np.ascontiguousarray(inp["nat_w_o"][0], np.float32)
    shared["btab"] = make_bias_tables(np.asarray(inp["nat_rpb"], np.float32)[0])
    a_re = np.asarray(inp["s5_a_re"], np.float32)[0]
    a_im = np.asarray(inp["s5_a_im"], np.float32)[0]
    ldt = np.asarray(inp["s5_log_dt"], np.float32)[0]
    s5a = np.zeros((128, 3, 64), np.float32)
    s5a[:, 0] = a_re.transpose(0, 2, 1).reshape(128, 64)
    s5a[:, 1] = a_im.transpose(0, 2, 1).reshape(128, 64)
    s5a[:, 2] = np.repeat(ldt[:, None, :], 64, axis=1).reshape(128, 64)
    shared["s5a"] = s5a.reshape(128, 192)
    s5bc = np.zeros((128, 4, 64, 16), np.float32)
    s5bc[:, 0] = np.asarray(inp["s5_b_re"], np.float32)[0].transpose(0, 2, 1, 3).reshape(128, 64, 16)
    s5bc[:, 1] = np.asarray(inp["s5_b_im"], np.float32)[0].transpose(0, 2, 1, 3).reshape(128, 64, 16)
    s5bc[:, 2] = np.asarray(inp["s5_c_re"], np.float32)[0].transpose(0, 3, 1, 2).reshape(128, 64, 16)
    s5bc[:, 3] = np.asarray(inp["s5_c_im"], np.float32)[0].transpose(0, 3, 1, 2).reshape(128, 64, 16)
    shared["s5bc"] = s5bc
    cst = np.zeros((128, 386), np.float32)
    cst[0:64, 384] = 1.0
    cst[64:128, 385] = 1.0
    cst[:, 0:128] = np.eye(128, dtype=np.float32)
    sp = np.arange(128) // 16
    cst[:, 128:256] = (sp[None, :] >= sp[:, None])
    cst[:, 256:384] = (sp[None, :] <= sp[:, None])
    shared["cst"] = cst
    selm = np.zeros((128, 64, 128), np.float32)
    for q in range(4):
        for par in range(2):
            for c in range(16):
                row = 32 * q + par * 16 + c
                for s_ in range(8):
                    selm[row, par * 8 + s_, s_ * 16 + c] = 1.0
                for gl in range(8):
                    selm[row, 16 + gl * 2 + par, gl * 16 + c] = 1.0
    selm[96:128, 32:64, :] = selm[96:128, 0:32, :]
    shared["sel"] = selm.reshape(128, 64 * 128)
    w = np.asarray(inp["ffn_w_up"], np.float32).reshape(DEPTH, KC, 128, 2, NFC, 128)
    shared["wup"] = np.ascontiguousarray(w.transpose(0, 4, 2, 1, 3, 5)).reshape(DEPTH, NFC, 128, KC * 256)
    return shared, vp.cols, vp.n


def prep_core(inp, bs):
    x = np.asarray(inp["x"], np.float32)[bs]
    ctx = np.asarray(inp["ctx"], np.float32)[bs]
    c = np.asarray(inp["c"], np.float32)[bs]
    nb = len(bs)
    cond = np.zeros((5, D), np.float32)
    cond[:nb] = c
    cond[4] = np.asarray(inp["c_ctx"], np.float32)
    condT = np.ascontiguousarray(cond.reshape(5, KC, 128).transpose(2, 1, 0)).reshape(128, KC * 5)
    return {
        "xT": np.ascontiguousarray(x.transpose(0, 2, 1)),
        "ctxT": np.ascontiguousarray(ctx.transpose(0, 2, 1)),
        "cond": condT,
    }


def run(inp, cfg, core_batches, trace=False):
    shared, cols, nvec = prep_shared(inp)
    nc = build_program(cfg, cols, nvec)
    in_maps = []
    for bs in core_batches:
        m = dict(shared)
        m.update(prep_core(inp, bs))
        in_maps.append(m)
    res = run_bass_kernel_spmd(nc, in_maps, core_ids=list(range(len(core_batches))), trace=trace)
    if trace:
        print("EXEC_TIME_NS", res.exec_time_ns)
    outs = [np.asarray(r["yT"]).transpose(0, 2, 1) for r in res.results]
    return np.ascontiguousarray(np.concatenate(outs, axis=0))


def kernel(**inputs):
    cfg = Cfg(nb=4, nl=4)
    core_batches = [list(range(4 * i, 4 * i + 4)) for i in range(8)]
    return run(inputs, cfg, core_batches).astype(np.float32)
```
